# Optimizing a Trainium2 kernel written in Bass

```python
import jax, jax.numpy as jnp
from jax import lax
import numpy as np

D_MODEL = 1024
BATCH = 4
SEQ = 4096
DEPTH = 1

LRU_WIDTH = D_MODEL
LRU_HEADS = 4
LRU_HEAD_DIM = LRU_WIDTH // LRU_HEADS
CONV_WIDTH = 4
LRU_C = 8.0
SGU_WIDTH = D_MODEL
SGU_GROUPS = 4
SGU_GROUP_DIM = SGU_WIDTH // SGU_GROUPS
CHUNK = 128
D_FF = 4 * D_MODEL
NORM_EPS = 1e-6
LN_EPS = 1e-5
OFF_XA = 0
OFF_GA = OFF_XA + LRU_WIDTH
OFF_U = OFF_GA + LRU_WIDTH
OFF_V = OFF_U + SGU_WIDTH
OFF_MA = OFF_V + SGU_WIDTH
OFF_MB = OFF_MA + D_MODEL
D_IN = OFF_MB + D_MODEL

kernel_name = "hybrid_rglru_sgu_gated_block"


def rms_norm(x, g):
    xf = x.astype(jnp.float32)
    y = xf * lax.rsqrt(jnp.mean(xf * xf, axis=-1, keepdims=True) + NORM_EPS)
    return (y * g.astype(jnp.float32)).astype(x.dtype)


def layer_norm(x, g, b):
    xf = x.astype(jnp.float32)
    mu = jnp.mean(xf, axis=-1, keepdims=True)
    var = jnp.mean(jnp.square(xf - mu), axis=-1, keepdims=True)
    y = (xf - mu) * lax.rsqrt(var + LN_EPS)
    return (y * g.astype(jnp.float32) + b.astype(jnp.float32)).astype(x.dtype)


def causal_depthwise_conv(x, w, b):
    k_w = w.shape[0]
    s = x.shape[1]
    xp = jnp.pad(x, ((0, 0), (k_w - 1, 0), (0, 0)))
    out = b
    for k in range(k_w):
        out = out + xp[:, k_w - 1 - k:k_w - 1 - k + s] * w[k]
    return out


def rg_lru(x, w_r, b_r, w_i, b_i, lam):
    bsz, s, c = x.shape
    xh = x.reshape(bsz, s, LRU_HEADS, LRU_HEAD_DIM)
    r = jax.nn.sigmoid(jnp.einsum('bshi,hij->bshj', xh, w_r) + b_r).reshape(bsz, s, c)
    i = jax.nn.sigmoid(jnp.einsum('bshi,hij->bshj', xh, w_i) + b_i).reshape(bsz, s, c)
    log_a = -LRU_C * r.astype(jnp.float32) * jax.nn.softplus(-lam.astype(jnp.float32))
    a = jnp.exp(log_a)
    mult = jnp.sqrt(-jnp.expm1(2.0 * log_a))
    bx = x.astype(jnp.float32) * i.astype(jnp.float32) * mult

    def combine(left, right):
        a_l, b_l = left
        a_r, b_r2 = right
        return a_l * a_r, a_r * b_l + b_r2

    _, h = lax.associative_scan(combine, (a, bx), axis=1)
    return h.astype(x.dtype)


def chunked_spatial_gating(u, v, ln_g, ln_b, w_s, b_s):
    bsz, s, c = v.shape
    n_chunks = s // CHUNK
    v = layer_norm(v, ln_g, ln_b)
    vc = v.reshape(bsz, n_chunks, CHUNK, SGU_GROUPS, SGU_GROUP_DIM)
    mask = jnp.tril(jnp.ones((CHUNK, CHUNK), dtype=w_s.dtype))
    sp = jnp.einsum('gts,bnsgc->bntgc', w_s * mask, vc) + jnp.transpose(b_s)[:, :, None]
    return u * sp.reshape(bsz, s, c)


def setup_inputs(seed: int = 0) -> dict:
    key = jax.random.key(seed)
    ks = jax.random.split(key, 24)
    f32 = jnp.float32
    nrm = lambda k, shape, scale: jax.random.normal(k, shape, f32) * scale
    x = jax.random.normal(ks[0], (BATCH, SEQ, D_MODEL), f32)
    norm_mix_g = 1.0 + nrm(ks[1], (DEPTH, D_MODEL), 0.02)
    w_in = nrm(ks[2], (DEPTH, D_MODEL, D_IN), D_MODEL ** -0.5)
    conv_w = nrm(ks[3], (DEPTH, CONV_WIDTH, LRU_WIDTH), CONV_WIDTH ** -0.5)
    conv_b = nrm(ks[4], (DEPTH, LRU_WIDTH), 0.01)
    w_rgate = nrm(ks[5], (DEPTH, LRU_HEADS, LRU_HEAD_DIM, LRU_HEAD_DIM), LRU_HEAD_DIM ** -0.5)
    b_rgate = nrm(ks[6], (DEPTH, LRU_HEADS, LRU_HEAD_DIM), 0.01)
    w_igate = nrm(ks[7], (DEPTH, LRU_HEADS, LRU_HEAD_DIM, LRU_HEAD_DIM), LRU_HEAD_DIM ** -0.5)
    b_igate = nrm(ks[8], (DEPTH, LRU_HEADS, LRU_HEAD_DIM), 0.01)
    a_c = jax.random.uniform(ks[9], (DEPTH, LRU_WIDTH), f32, 0.9, 0.999)
    sig = a_c ** (1.0 / LRU_C)
    lru_lambda = jnp.log(sig) - jnp.log1p(-sig)
    w_out_a = nrm(ks[10], (DEPTH, LRU_WIDTH, D_MODEL), LRU_WIDTH ** -0.5)
    sgu_ln_g = 1.0 + nrm(ks[11], (DEPTH, SGU_WIDTH), 0.02)
    sgu_ln_b = nrm(ks[12], (DEPTH, SGU_WIDTH), 0.01)
    sgu_w_s = nrm(ks[13], (DEPTH, SGU_GROUPS, CHUNK, CHUNK), CHUNK ** -0.5)
    sgu_b_s = 1.0 + nrm(ks[14], (DEPTH, SGU_GROUPS, CHUNK), 0.01)
    w_out_b = nrm(ks[15], (DEPTH, SGU_WIDTH, D_MODEL), SGU_WIDTH ** -0.5)
    w_out = nrm(ks[16], (DEPTH, D_MODEL, D_MODEL), D_MODEL ** -0.5)
    norm_mlp_g = 1.0 + nrm(ks[17], (DEPTH, D_MODEL), 0.02)
    w_up = nrm(ks[18], (DEPTH, D_MODEL, D_FF), D_MODEL ** -0.5)
    w_down = nrm(ks[19], (DEPTH, D_FF, D_MODEL), D_FF ** -0.5)
    norm_final_g = 1.0 + nrm(ks[20], (D_MODEL,), 0.02)
    return {"x": x, "norm_mix_g": norm_mix_g, "w_in": w_in, "conv_w": conv_w, "conv_b": conv_b,
            "w_rgate": w_rgate, "b_rgate": b_rgate, "w_igate": w_igate, "b_igate": b_igate,
            "lru_lambda": lru_lambda, "w_out_a": w_out_a, "sgu_ln_g": sgu_ln_g, "sgu_ln_b": sgu_ln_b,
            "sgu_w_s": sgu_w_s, "sgu_b_s": sgu_b_s, "w_out_b": w_out_b, "w_out": w_out,
            "norm_mlp_g": norm_mlp_g, "w_up": w_up, "w_down": w_down, "norm_final_g": norm_final_g}


def reference(x, norm_mix_g, w_in, conv_w, conv_b, w_rgate, b_rgate, w_igate, b_igate,
              lru_lambda, w_out_a, sgu_ln_g, sgu_ln_b, sgu_w_s, sgu_b_s, w_out_b, w_out,
              norm_mlp_g, w_up, w_down, norm_final_g):
    h = x
    for l in range(DEPTH):
        n = rms_norm(h, norm_mix_g[l])
        z = n @ w_in[l]
        xa = z[..., OFF_XA:OFF_GA]
        ga = z[..., OFF_GA:OFF_U]
        ub = z[..., OFF_U:OFF_V]
        vb = z[..., OFF_V:OFF_MA]
        ma = z[..., OFF_MA:OFF_MB]
        mb = z[..., OFF_MB:D_IN]
        xa = causal_depthwise_conv(xa, conv_w[l], conv_b[l])
        ya = rg_lru(xa, w_rgate[l], b_rgate[l], w_igate[l], b_igate[l], lru_lambda[l]) * jax.nn.gelu(ga)
        yb = chunked_spatial_gating(jax.nn.gelu(ub), jax.nn.gelu(vb), sgu_ln_g[l], sgu_ln_b[l],
                                    sgu_w_s[l], sgu_b_s[l])
        merged = jax.nn.sigmoid(ma) * (ya @ w_out_a[l]) + jax.nn.sigmoid(mb) * (yb @ w_out_b[l])
        h = h + merged @ w_out[l]
        n2 = rms_norm(h, norm_mlp_g[l])
        h = h + jnp.square(jax.nn.relu(n2 @ w_up[l])) @ w_down[l]
    return rms_norm(h, norm_final_g)
```

```python
import numpy as np
import concourse.bass as bass
import concourse.mybir as mybir
from concourse.bass_utils import run_bass_kernel_spmd

F32 = mybir.dt.float32
BF16 = mybir.dt.bfloat16
AF = mybir.ActivationFunctionType
ALU = mybir.AluOpType

D = 1024
SEQ = 4096
BATCH = 4
NCORES = 8
TCORE = 2048
TB = 1024
NTT = TB // 128
NSUB = TB // 512
DFF = 4096
NSLOT = 6
N_T32 = 12
NORM_EPS = 1e-6
LN_EPS = 1e-5

PV_CW = 0
PV_CB = 32
PV_BR = 40
PV_BI = 48
PV_LAM = 56
PV_FLAG = 64
PV_N = 65
BV_G1, BV_G2, BV_GF, BV_LNG, BV_LNB = range(5)


class Ev:
    __slots__ = ("sem", "val")

    def __init__(self, sem, val):
        self.sem = sem
        self.val = val


class Tok:
    __slots__ = ("w", "r", "name")

    def __init__(self, name=""):
        self.w = None
        self.r = []
        self.name = name


class Eng:
    def __init__(self, name, sem):
        self.name = name
        self.sem = sem
        self.cnt = 0
        self.waited = {}
        self.q = []
        self.pend_r = []
        self.pend_w = []


class DmaSem:
    def __init__(self, h):
        self.h = h
        self.cnt = 0


class Prog:
    def __init__(self, nc):
        self.nc = nc
        self.eng = {}
        for n in ("pe", "act", "dve", "pool", "sp"):
            self.eng[n] = Eng(n, nc.alloc_semaphore(name="s_" + n))
        self.nsem = 5

    def dsem(self, name):
        self.nsem += 1
        return DmaSem(self.nc.alloc_semaphore(name=name))

    def _waits(self, E, r, w):
        deps = []
        for t in r:
            if t.w is not None:
                deps.append(t.w)
        for t in w:
            if t.w is not None:
                deps.append(t.w)
            deps.extend(t.r)
        best = {}
        for ev in deps:
            assert ev.val is not None
            if ev.sem is E.sem and E.name == "pe":
                continue
            if E.waited.get(ev.sem, 0) < ev.val:
                if ev.sem not in best or best[ev.sem].val < ev.val:
                    best[ev.sem] = ev
        for sem, ev in best.items():
            E.waited[sem] = ev.val
        return list(best.values())

    def op(self, e, fn, r=(), w=(), sig=True):
        E = self.eng[e]
        waits = self._waits(E, r, w)
        if not sig:
            E.pend_r.extend(r)
            E.pend_w.extend(w)
            E.q.append((waits, fn, None))
            return None
        E.cnt += 1
        ev = Ev(E.sem, E.cnt)
        E.q.append((waits, fn, (E.sem, 1)))
        for t in list(r) + E.pend_r:
            t.r.append(ev)
        for t in list(w) + E.pend_w:
            t.w = ev
            t.r = []
        E.pend_r = []
        E.pend_w = []
        return ev

    def dma(self, q, fn, sem, r=(), w=()):
        E = self.eng[q]
        waits = self._waits(E, r, w)
        sem.cnt += 16
        ev = Ev(sem.h, sem.cnt)
        E.q.append((waits, fn, (sem.h, 16)))
        for t in r:
            t.r.append(ev)
        for t in w:
            t.w = ev
            t.r = []
        return ev

    def wait_all(self, q, evs):
        E = self.eng[q]
        waits = []
        for ev in evs:
            if E.waited.get(ev.sem, 0) < ev.val:
                E.waited[ev.sem] = ev.val
                waits.append(ev)
        E.q.append((waits, None, None))

    @staticmethod
    def run(E, e):
        for waits, fn, post in E.q:
            for ev in waits:
                e.wait_ge(ev.sem, ev.val)
            if fn is None:
                continue
            ins = fn(e)
            if post is not None:
                ins.then_inc(post[0], post[1])


class Tile:
    def __init__(self, h, ntok=1, name=""):
        self.h = h
        self.t = [Tok(name + str(i)) for i in range(ntok)]

    @property
    def tok(self):
        return self.t[0]


class Ring:
    def __init__(self, items):
        self.items = items
        self.i = 0

    def get(self):
        x = self.items[self.i % len(self.items)]
        self.i += 1
        return x


def build_program():
    nc = bass.Bass("TRN2", target_bir_lowering=False)
    P = Prog(nc)

    def din(name, shape):
        return nc.dram_tensor(name, list(shape), F32, kind="ExternalInput")

    x_main = din("x_main", [TCORE, D])
    x_pre = din("x_pre", [TCORE, D])
    w_in = din("w_in", [D, 6 * D])
    gates_d = din("gates_p", [128, 4096])
    w_oa = din("w_out_a", [D, D])
    w_ob = din("w_out_b", [D, D])
    w_o = din("w_out", [D, D])
    w_up = din("w_up", [D, DFF])
    w_dn = din("w_down", [DFF, D])
    w_s = din("sgu_w_s", [4, 128, 128])
    pvec_d = din("pvec", [128, PV_N])
    bvec_d = din("bvec", [5 * 128, D])
    bsrow_d = din("bsrow", [1, 512])
    tril_d = din("tril", [128, 128])
    ident_d = din("ident", [128, 128])
    out_d = nc.dram_tensor("out", [TCORE, D], F32, kind="ExternalOutput")

    def sb(name, shape, dt, ntok=1):
        return Tile(nc.alloc_sbuf_tensor(name, list(shape), dt), ntok, name)

    h_tok = sb("h_tok", [128, NTT, D], F32, NTT)
    n_fm = sb("n_fm", [128, 8, TB], BF16, NTT)
    wslots = [sb(f"wslot{i}", [128, 8, 512], BF16) for i in range(NSLOT)]
    wsems = [P.dsem(f"d_w{i}") for i in range(NSLOT)]
    bc = [sb(f"bc{i}", [128, D], F32) for i in range(3)]
    bcsems = [P.dsem(f"d_bc{i}") for i in range(3)]
    bcr = {"a": 0, "b": 1, "c": 2}
    pvec = sb("pvec_sb", [128, PV_N], F32)
    pv2 = sb("pv2", [128, 40], F32)
    ident16 = sb("ident16", [128, 128], BF16)
    negh = sb("negh", [128, 8], F32)
    wmt16 = sb("wmt16", [128, 4, 128], BF16)
    ones16 = sb("ones16", [2, 128], BF16)
    bs_hl = sb("bs_hl", [2, 512], BF16)
    bs_hi = sb("bs_hi", [2, 512], BF16)
    state = sb("state", [128, 8], F32, 8)
    halo = sb("halo", [128, 8, 2, 3], F32, 16)
    stat = sb("stat", [128, 16], F32)
    nstat = sb("nstat", [128, 32], F32, 8)
    stat_ring = Ring([0, 8])
    stat_tok = [Tok("st0"), Tok("st1")]
    junk16 = [sb(f"junk{i}", [128, D], BF16) for i in range(1)]
    junk_ring = Ring(junk16)

    big = sb("big", [128, 16, TB], BF16, 32)
    a16 = big
    gm = sb("gm", [128, 8, TB], BF16, 16)
    t32 = [sb(f"t32_{i}", [128, 512], F32) for i in range(N_T32)]
    t32_ring = Ring(t32)
    xc16 = [sb(f"xc16_{i}", [128, 512], BF16) for i in range(4)]
    xcsems = [P.dsem(f"d_xc{i}") for i in range(4)]
    xc16_ring = Ring(xc16)
    gv32 = [sb(f"gv32_{i}", [128, D], F32, 2) for i in range(2)]
    gv_ring = Ring(gv32)
    vln16 = [sb(f"vln16_{i}", [128, D], BF16) for i in range(4)]
    vln_ring = Ring(vln16)
    xn_ring = Ring(vln16)
    gm32 = gm.h.bitcast(F32)

    class V:
        def __init__(self, h, tok):
            self.h = h
            self.t = [tok]
            self.tok = tok
    xtiles = list(t32)
    for gvt in gv32:
        xtiles.append(V(gvt.h[:, 0:512], gvt.t[0]))
        xtiles.append(V(gvt.h[:, 512:1024], gvt.t[1]))
    for vt in vln16:
        xtiles.append(V(vt.h.bitcast(F32), vt.tok))
    x_ring = Ring(xtiles)
    sub_ctr = [0] * 8
    osems = [P.dsem(f"d_o{i}") for i in range(4)]
    ostage_ring = Ring(list(range(4)))
    xsems = [P.dsem(f"d_x{i}") for i in range(NTT)]
    csem = P.dsem("d_const")

    banks = [Tile(nc.alloc_psum_tensor(f"bank{i}", [128, 512], F32), 1, f"bank{i}") for i in range(8)]
    bank_ring = Ring(banks)

    print("sbuf bytes remaining/partition:", nc.sbuf_bytes_remaining, " sems used:", P.nsem)

    def ACT(fn, r=(), w=()):
        return P.op("act", fn, r, w)

    def DVE(fn, r=(), w=()):
        return P.op("dve", fn, r, w)

    def POOL(fn, r=(), w=()):
        return P.op("pool", fn, r, w)

    def PE(fn, r=(), w=(), sig=True):
        return P.op("pe", fn, r, w, sig)

    def pcol(c, n=1):
        return pvec.h[:, c:c + n]

    wq = []
    wq_ptr = [0]
    w_issued = {}
    slot_ctr = [0]

    def w_in_src(seg, half):
        c0 = seg * 1024 + half * 512
        return lambda: w_in.ap().rearrange("(kt p) c -> p kt c", p=128)[:, :, c0:c0 + 512]

    def sq_src(t, half):
        return lambda: t.ap().rearrange("(kt p) c -> p kt c", p=128)[:, :, half * 512:(half + 1) * 512]

    def up_src(j):
        return lambda: w_up.ap().rearrange("(kt p) c -> p kt c", p=128)[:, :, j * 512:(j + 1) * 512]

    def dn_src(kt0, half):
        return lambda: w_dn.ap().rearrange("(kt p) c -> p kt c", p=128)[:, kt0:kt0 + 8, half * 512:(half + 1) * 512]

    def w_prefetch(n_ahead, extra_r=()):
        while wq_ptr[0] < len(wq) and (wq_ptr[0] - w_consumed[0]) < n_ahead:
            key, srcf = wq[wq_ptr[0]]
            si = slot_ctr[0] % NSLOT
            slot_ctr[0] += 1
            slot = wslots[si]
            src = srcf()
            P.dma("pool", (lambda e, s=slot, a=src: e.dma_start(out=s.h[:], in_=a)), wsems[si], r=tuple(extra_r),
                  w=(slot.tok,))
            w_issued[key] = si
            wq_ptr[0] += 1

    w_consumed = [0]

    def w_get(key):
        if key not in w_issued:
            while key not in w_issued:
                assert wq_ptr[0] < len(wq), key
                w_prefetch(wq_ptr[0] - w_consumed[0] + 1)
        return wslots[w_issued[key]]

    def w_done(nkeys=1):
        w_consumed[0] += nkeys
        w_prefetch(NSLOT)

    SEG_XA, SEG_GA, SEG_U, SEG_V, SEG_MA, SEG_MB = range(6)
    blocks = [("pre", 0), ("main", 0), ("main", 1)]
    for kind, bi in blocks:
        tag = f"{kind}{bi}"
        if kind == "main":
            for hf in range(2):
                wq.append(((tag, "u", hf), w_in_src(SEG_U, hf)))
            for hf in range(2):
                wq.append(((tag, "v", hf), w_in_src(SEG_V, hf)))
            for hf in range(2):
                wq.append(((tag, "ga", hf), w_in_src(SEG_GA, hf)))
        for hf in range(2):
            wq.append(((tag, "xa", hf), w_in_src(SEG_XA, hf)))
        wq.append(((tag, "gates"), (lambda: gates_d.ap().rearrange("p (j m) -> p j m", j=8))))
        if kind == "main":
            for hf in range(2):
                wq.append(((tag, "ma", hf), w_in_src(SEG_MA, hf)))
                wq.append(((tag, "oa", hf), sq_src(w_oa, hf)))
                wq.append(((tag, "mb", hf), w_in_src(SEG_MB, hf)))
                wq.append(((tag, "ob", hf), sq_src(w_ob, hf)))
            for hf in range(2):
                wq.append(((tag, "o", hf), sq_src(w_o, hf)))
            for dh in range(2):
                for j in range(4):
                    wq.append(((tag, "up", dh * 4 + j), up_src(dh * 4 + j)))
                for hf in range(2):
                    for q in range(2):
                        wq.append(((tag, "dn", dh, hf, q), dn_src(dh * 16 + q * 8, hf)))

    const_evs = []

    def const_dma(q, out_ap, in_ap, wtoks):
        ev = P.dma(q, (lambda e: e.dma_start(out=out_ap, in_=in_ap)), csem, r=(), w=wtoks)
        assert all(wv.sem is not csem.h for wv in P.eng[q].q[-1][0]), "const DMA must not depend on a const DMA"
        const_evs.append(ev)
        return ev

    ident32 = V(t32[0].h[:, 0:128], t32[0].tok)
    tril32 = V(t32[1].h[:, 0:128], t32[1].tok)
    ws32 = V(t32[2].h.rearrange("p (g s) -> p g s", g=4), t32[2].tok)
    ws16 = V(xc16[0].h.rearrange("p (g s) -> p g s", g=4), xc16[0].tok)
    bs32 = V(t32[3].h[0:2, :], t32[3].tok)
    bsb32 = V(t32[4].h[0:2, :], t32[4].tok)

    def load_bc(i, row):
        src = bvec_d.ap()[row * 128:(row + 1) * 128, :]
        P.dma("sp", (lambda e: e.dma_start(out=bc[i].h[:], in_=src)), bcsems[i], r=(), w=(bc[i].tok,))

    isem = P.dsem("d_ident")
    P.dma("sp", (lambda e: e.dma_start(out=ident32.h[:], in_=ident_d.ap())), isem, r=(), w=(ident32.tok,))
    for tt in range(NTT):
        P.dma("sp", (lambda e, tt=tt: e.dma_start(out=h_tok.h[:, tt, :], in_=x_pre.ap()[tt * 128:(tt + 1) * 128, :])),
              xsems[tt], r=(), w=(h_tok.t[tt],))
    load_bc(0, BV_G1)
    bigf = big.h.bitcast(F32)
    pre2_src = []
    x2sems = [P.dsem(f"d_x2_{i}") for i in range(NTT)]
    for i in range(NTT):
        if i < 4:
            ap3 = gm32[:, 2 * i:2 * i + 2, :]
            toks = tuple(gm.t[4 * i:4 * i + 4])
        else:
            j = i - 4
            ap3 = bigf[:, 8 + 2 * j:8 + 2 * j + 2, :]
            toks = tuple(big.t[16 + 4 * j:16 + 4 * j + 4])
        r0 = TB + i * 128
        P.dma("sp", (lambda e, ap3=ap3, r0=r0: e.dma_start(
            out=ap3, in_=x_pre.ap()[r0:r0 + 128, :].rearrange("p (a b) -> p a b", a=2))), x2sems[i], r=(), w=toks)
        pre2_src.append((ap3.rearrange("p a b -> p (a b)"), toks))
    const_dma("sp", pvec.h[:], pvec_d.ap(), (pvec.tok,))
    const_dma("sp", tril32.h[:], tril_d.ap(), (tril32.tok,))
    const_dma("sp", ws32.h[:], w_s.ap().rearrange("g t s -> t g s"), (ws32.tok,))
    bs32b_tok = Tok("bs32b")
    const_dma("sp", bs32.h[0:1, :], bsrow_d.ap(), (bs32.tok,))
    const_dma("sp", bs32.h[1:2, :], bsrow_d.ap(), (bs32b_tok,))
    for ev in const_evs:
        ev.val = csem.cnt

    DVE(lambda e: e.tensor_copy(out=ident16.h[:], in_=ident32.h[:]), r=(ident32.tok,), w=(ident16.tok,))
    DVE(lambda e: e.memset(negh.h[:], -0.5), w=(negh.tok,))
    HBR, HBI, SC, HSC = 0, 8, 16, 24
    pv3 = sb("pv3", [128, 24], F32)

    def setup_compute():
      if True:
        DVE(lambda e: e.tensor_scalar(out=pv2.h[:, 0:16], in0=pvec.h[:, PV_BR:PV_BR + 16], scalar1=0.5, scalar2=None,
                                      op0=ALU.mult), r=(pvec.tok,), w=(pv2.tok,))
        ACT(lambda e: e.activation(out=pv2.h[:, 32:40], in_=pvec.h[:, PV_LAM:PV_LAM + 8], func=AF.Exp, scale=-1.0),
            r=(pvec.tok, pv2.tok), w=(pv2.tok,))
        DVE(lambda e: e.tensor_scalar(out=pv3.h[:, 0:8], in0=pv2.h[:, 32:40], scalar1=1.0, scalar2=None, op0=ALU.add),
            r=(pv2.tok,), w=(pv3.tok,))
        DVE(lambda e: e.tensor_scalar(out=pv3.h[:, 8:16], in0=pv3.h[:, 0:8], scalar1=-1.0, scalar2=1e-30, op0=ALU.add,
                                      op1=ALU.max), r=(pv3.tok,), w=(pv3.tok,))
        DVE(lambda e: e.reciprocal(out=pv3.h[:, 8:16], in_=pv3.h[:, 8:16]), r=(pv3.tok,), w=(pv3.tok,))
        ACT(lambda e: e.activation(out=pv3.h[:, 16:24], in_=pv3.h[:, 0:8], func=AF.Ln),
            r=(pv3.tok,), w=(pv3.tok,))
        DVE(lambda e: e.tensor_tensor(out=pv3.h[:, 16:24], in0=pv3.h[:, 16:24], in1=pv3.h[:, 8:16], op=ALU.mult),
            r=(pv3.tok,), w=(pv3.tok,))
        DVE(lambda e: e.tensor_tensor(out=pv2.h[:, 32:40], in0=pv3.h[:, 16:24], in1=pv2.h[:, 32:40], op=ALU.mult),
            r=(pv3.tok, pv2.tok), w=(pv2.tok,))
        DVE(lambda e: e.tensor_scalar(out=pv2.h[:, 16:24], in0=pv2.h[:, 32:40], scalar1=-8.0, scalar2=None, op0=ALU.mult),
            r=(pv2.tok,), w=(pv2.tok,))
        DVE(lambda e: e.tensor_scalar(out=pv2.h[:, 24:32], in0=pv2.h[:, 32:40], scalar1=-4.0, scalar2=None, op0=ALU.mult),
            r=(pv2.tok,), w=(pv2.tok,))

        for g in range(4):
            DVE(lambda e, g=g: e.tensor_tensor(out=ws16.h[:, g, :], in0=ws32.h[:, g, :], in1=tril32.h[:], op=ALU.mult),
                r=(ws32.tok, tril32.tok), w=(ws16.tok,))
        for g in range(4):
            bk = bank_ring.get()
            bv = bk.h.bitcast(BF16)
            PE(lambda e, g=g, bv=bv: e.transpose(bv[:, 0:128], ws16.h[:, g, :], ident16.h[:]),
               r=(ws16.tok, ident16.tok), w=(bk.tok,))
            ACT(lambda e, g=g, bv=bv: e.activation(out=wmt16.h[:, g, :], in_=bv[:, 0:128], func=AF.Copy),
                r=(bk.tok,), w=(wmt16.tok,))
        DVE(lambda e: e.tensor_copy(out=bs_hi.h[:], in_=bs32.h[:]), r=(bs32.tok, bs32b_tok), w=(bs_hi.tok,))
        DVE(lambda e: e.tensor_copy(out=bsb32.h[:], in_=bs_hi.h[:]), r=(bs_hi.tok,), w=(bsb32.tok,))
        DVE(lambda e: e.tensor_tensor(out=bsb32.h[:], in0=bs32.h[:], in1=bsb32.h[:], op=ALU.subtract),
            r=(bs32.tok, bs32b_tok, bsb32.tok), w=(bsb32.tok,))
        DVE(lambda e: e.tensor_copy(out=bs_hl.h[:], in_=bsb32.h[:]), r=(bsb32.tok,), w=(bs_hl.tok,))
        DVE(lambda e: e.tensor_copy(out=bs_hl.h[0:1, :], in_=bs_hi.h[0:1, :]), r=(bs_hi.tok, bs_hl.tok), w=(bs_hl.tok,))
        DVE(lambda e: e.memset(ones16.h[:], 1.0), w=(ones16.tok,))
        DVE(lambda e: e.memset(state.h[:], 0.0), w=tuple(state.t))
        DVE(lambda e: e.memset(halo.h[:], 0.0), w=tuple(halo.t))

    def load_x_tile(xsrc, blk, tt):
        r0 = blk * TB + tt * 128
        src = xsrc.ap()[r0:r0 + 128, :]
        P.dma("sp", (lambda e, tt=tt, a=src: e.dma_start(out=h_tok.h[:, tt, :], in_=a)), xsems[tt],
              r=(), w=(h_tok.t[tt],))

    def stage_load_x(xsrc, blk):
        for tt in range(NTT):
            load_x_tile(xsrc, blk, tt)

    def rms_stats(tt, src=None):
        base = tt * 4
        st = nstat.t[tt]
        jk = junk_ring.get()
        sap, stk = src if src is not None else (h_tok.h[:, tt, :], (h_tok.t[tt],))
        ACT(lambda e: e.activation(out=jk.h[:], in_=sap, func=AF.Square,
                                   accum_out=nstat.h[:, base:base + 1]),
            r=tuple(stk), w=(jk.tok, st))
        DVE(lambda e: e.tensor_scalar(out=nstat.h[:, base + 1:base + 2], in0=nstat.h[:, base:base + 1],
                                      scalar1=1.0 / D, scalar2=NORM_EPS, op0=ALU.mult, op1=ALU.add),
            r=(st,), w=(st,))
        POOL(lambda e: e.tensor_tensor(out=nstat.h[:, base + 2:base + 3], in0=nstat.h[:, base + 1:base + 2],
                                       in1=negh.h[:, 0:1], op=ALU.pow),
             r=(st, negh.tok), w=(st,))
        return nstat.h[:, base + 2:base + 3], st

    def norm_emitter(bci, to_big=False, pre=None, srcs=None):
        dst = big if to_big else n_fm
        def dtoks(tt):
            return tuple(big.t[k * 2 + tt // 4] for k in range(8)) if to_big else (n_fm.t[tt],)
        sts = {}
        trs = {}
        kk = [0]

        def step():
            k = kk[0]
            kk[0] += 1
            if k < NTT:
                sts[k] = pre[k] if pre is not None else rms_stats(k, srcs[k] if srcs is not None else None)
            if 0 <= k - 1 < NTT:
                tt = k - 1
                col, st = sts.pop(tt)
                xn = xn_ring.get()
                sap, stk = srcs[tt] if srcs is not None else (h_tok.h[:, tt, :], (h_tok.t[tt],))
                DVE(lambda e, xn=xn, col=col, sap=sap: e.scalar_tensor_tensor(
                    out=xn.h[:], in0=sap, scalar=col, in1=bc[bci].h[:],
                    op0=ALU.mult, op1=ALU.mult),
                    r=tuple(stk) + (st, bc[bci].tok), w=(xn.tok,))
                bk = bank_ring.get()
                bv = bk.h.bitcast(BF16)
                for kt in range(8):
                    PE(lambda e, xn=xn, bv=bv, kt=kt: e.transpose(bv[:, kt * 128:(kt + 1) * 128],
                                                                 xn.h[:, kt * 128:(kt + 1) * 128], ident16.h[:]),
                       r=(xn.tok, ident16.tok), w=(bk.tok,), sig=(kt == 7))
                trs[tt] = (bk, bv)
            if 0 <= k - 2 < NTT:
                tt = k - 2
                bk, bv = trs.pop(tt)
                bv3 = bv.rearrange("p (k t) -> p k t", k=8)
                ACT(lambda e, bv3=bv3, tt=tt: e.activation(
                    out=dst.h[:, 0:4, tt * 128:(tt + 1) * 128], in_=bv3[:, 0:4, :], func=AF.Copy),
                    r=(bk.tok,), w=dtoks(tt))
                DVE(lambda e, bv3=bv3, tt=tt: e.tensor_copy(
                    out=dst.h[:, 4:8, tt * 128:(tt + 1) * 128], in_=bv3[:, 4:8, :]),
                    r=(bk.tok,), w=dtoks(tt))
        return step

    def stage_norm_to_fm(bci, to_big=False, srcs=None):
        step = norm_emitter(bci, to_big, srcs=srcs)
        for _ in range(NTT + 2):
            step()

    def n_toks(sub):
        return tuple(n_fm.t[sub * 4:(sub + 1) * 4])

    def mm_fm(wslot, ct_in, sub, rhs_tile=None, rhs_toks=None, kt_off=0):
        bk = bank_ring.get()
        for kt in range(8):
            if rhs_tile is None:
                rhs = n_fm.h[:, kt, sub * 512:(sub + 1) * 512]
                rt = n_toks(sub)
            else:
                rhs = rhs_tile.h[:, kt_off + kt, sub * 512:(sub + 1) * 512]
                rt = rhs_toks
            PE(lambda e, bk=bk, kt=kt, rhs=rhs: e.matmul(
                bk.h[:], lhsT=wslot.h[:, kt, ct_in * 128:(ct_in + 1) * 128], rhs=rhs,
                start=(kt == 0), stop=(kt == 7)),
               r=(wslot.tok,) + tuple(rt), w=(bk.tok,), sig=(kt == 7))
        return bk

    def stage_ga(tag):
        for hf in range(2):
            ws = w_get((tag, "ga", hf))
            for sub in range(NSUB):
                for ci in range(4):
                    ct = hf * 4 + ci
                    bk = mm_fm(ws, ci, sub)
                    ACT(lambda e, bk=bk, ct=ct, sub=sub: e.activation(
                        out=gm.h[:, ct, sub * 512:(sub + 1) * 512], in_=bk.h[:], func=AF.Gelu_apprx_tanh),
                        r=(bk.tok,), w=(gm.t[ct * 2 + sub],))
            w_done()

    def stage_b(tag):
        bcg = bc[bcr["b"]]
        bcb = bc[bcr["c"]]
        wu = [w_get((tag, "u", hf)) for hf in range(2)]
        wv = [w_get((tag, "v", hf)) for hf in range(2)]

        def v_front(tt):
            gv = gv_ring.get()
            gi = stat_ring.i % 2
            base = stat_ring.get()
            st = stat_tok[gi]
            for hf in range(2):
                bk = bank_ring.get()
                for kt in range(8):
                    PE(lambda e, bk=bk, kt=kt, wvh=wv[hf], tt=tt: e.matmul(
                        bk.h[:], lhsT=n_fm.h[:, kt, tt * 128:(tt + 1) * 128], rhs=wvh.h[:, kt, :],
                        start=(kt == 0), stop=(kt == 7)),
                       r=(wv[hf].tok, n_fm.t[tt]), w=(bk.tok,), sig=(kt == 7))
                ACT(lambda e, bk=bk, gv=gv, hf=hf, base=base: e.activation(
                    out=gv.h[:, hf * 512:(hf + 1) * 512], in_=bk.h[:], func=AF.Gelu_apprx_tanh,
                    accum_out=stat.h[:, base + hf:base + hf + 1]),
                    r=(bk.tok,), w=(gv.t[hf], st))
            jk = junk_ring.get()
            ACT(lambda e, gv=gv, jk=jk, base=base: e.activation(
                out=jk.h[:], in_=gv.h[:], func=AF.Square, accum_out=stat.h[:, base + 2:base + 3]),
                r=tuple(gv.t), w=(jk.tok, st))
            b = base
            DVE(lambda e, b=b: e.tensor_tensor(out=stat.h[:, b + 3:b + 4], in0=stat.h[:, b:b + 1],
                                               in1=stat.h[:, b + 1:b + 2], op=ALU.add), r=(st,), w=(st,))
            DVE(lambda e, b=b: e.tensor_scalar(out=stat.h[:, b + 3:b + 4], in0=stat.h[:, b + 3:b + 4],
                                               scalar1=1.0 / D, scalar2=None, op0=ALU.mult), r=(st,), w=(st,))
            DVE(lambda e, b=b: e.tensor_tensor(out=stat.h[:, b + 4:b + 5], in0=stat.h[:, b + 3:b + 4],
                                               in1=stat.h[:, b + 3:b + 4], op=ALU.mult), r=(st,), w=(st,))
            DVE(lambda e, b=b: e.scalar_tensor_tensor(out=stat.h[:, b + 5:b + 6], in0=stat.h[:, b + 2:b + 3],
                                                      scalar=1.0 / D, in1=stat.h[:, b + 4:b + 5],
                                                      op0=ALU.mult, op1=ALU.subtract), r=(st,), w=(st,))
            DVE(lambda e, b=b: e.tensor_scalar(out=stat.h[:, b + 5:b + 6], in0=stat.h[:, b + 5:b + 6],
                                               scalar1=LN_EPS, scalar2=None, op0=ALU.add),
                r=(st,), w=(st,))
            POOL(lambda e, b=b: e.tensor_tensor(out=stat.h[:, b + 6:b + 7], in0=stat.h[:, b + 5:b + 6],
                                                in1=negh.h[:, 0:1], op=ALU.pow),
                 r=(st, negh.tok), w=(st,))
            DVE(lambda e, b=b: e.scalar_tensor_tensor(out=stat.h[:, b + 7:b + 8], in0=stat.h[:, b + 3:b + 4],
                                                      scalar=-1.0, in1=stat.h[:, b + 6:b + 7],
                                                      op0=ALU.mult, op1=ALU.mult), r=(st,), w=(st,))
            return (gv, b, st)

        def v_back(ctx):
            gv, b, st = ctx
            ACT(lambda e, gv=gv, b=b: e.activation(out=gv.h[:], in_=gv.h[:], func=AF.Identity,
                                                   scale=stat.h[:, b + 6:b + 7], bias=stat.h[:, b + 7:b + 8]),
                r=tuple(gv.t) + (st,), w=tuple(gv.t))
            DVE(lambda e, gv=gv: e.tensor_tensor(out=gv.h[:], in0=gv.h[:], in1=bcg.h[:], op=ALU.mult),
                r=tuple(gv.t) + (bcg.tok,), w=tuple(gv.t))
            v16 = vln_ring.get()
            POOL(lambda e, gv=gv, v16=v16: e.tensor_tensor(out=v16.h[:], in0=gv.h[:], in1=bcb.h[:], op=ALU.add),
                 r=tuple(gv.t) + (bcb.tok,), w=(v16.tok,))
            return v16

        for sub in range(NSUB):
            vl = []
            ctxs = {}
            for j in range(5):
                if j < 4:
                    ctxs[j] = v_front(sub * 4 + j)
                if j - 1 >= 0:
                    vl.append(v_back(ctxs.pop(j - 1)))
            gu = []
            for ct in range(8):
                bk = mm_fm(wu[ct // 4], ct % 4, sub)
                t = t32_ring.get()
                ACT(lambda e, bk=bk, t=t: e.activation(out=t.h[:], in_=bk.h[:], func=AF.Gelu_apprx_tanh),
                    r=(bk.tok,), w=(t.tok,))
                gu.append(t)
            for ct in range(8):
                g = ct // 2
                bk = bank_ring.get()
                brow = bass.AP(bs_hl.h, g * 128, [[512, 2], [0, 4], [1, 128]])
                PE(lambda e, bk=bk, brow=brow: e.matmul(
                    bk.h[:].rearrange("p (a b) -> p a b", a=4), lhsT=ones16.h[0:2, :], rhs=brow,
                    start=True, stop=False),
                   r=(ones16.tok, bs_hl.tok), w=(bk.tok,), sig=False)
                for j in range(4):
                    o = bk.h[:, j * 128:(j + 1) * 128]
                    PE(lambda e, o=o, v=vl[j], ct=ct, g=g, j=j: e.matmul(
                        o, lhsT=v.h[:, ct * 128:(ct + 1) * 128], rhs=wmt16.h[:, g, :], start=False, stop=(j == 3)),
                       r=(vl[j].tok, wmt16.tok), w=(bk.tok,), sig=(j == 3))
                DVE(lambda e, bk=bk, ct=ct, sub=sub, gt=gu[ct]: e.tensor_tensor(
                    out=big.h[:, 8 + ct, sub * 512:(sub + 1) * 512], in0=bk.h[:], in1=gt.h[:], op=ALU.mult),
                    r=(bk.tok, gu[ct].tok), w=(big.t[16 + ct * 2 + sub],))
        w_done(4)

    def stage_xa(tag, main, nsub=NSUB):
        wg = w_get((tag, "gates"))
        cwf = lambda k, ct: pcol(PV_CW + k * 8 + ct)

        def p1_front(hd, sub):
            ws = w_get((tag, "xa", hd // 2))
            u = {"hd": hd, "sub": sub, "t": [], "bk": [], "hin": [], "x16": []}
            for c2 in range(2):
                ct = hd * 2 + c2
                par = sub_ctr[ct] % 2
                sub_ctr[ct] += 1
                hin = halo.h[:, ct, par, :]
                hin_t = halo.t[ct * 2 + par]
                hout = halo.h[:, ct, 1 - par, :]
                hout_t = halo.t[ct * 2 + 1 - par]
                if sub < 2:
                    bk = mm_fm(ws, (hd % 2) * 2 + c2, sub)
                else:
                    bk = mm_fm(ws, (hd % 2) * 2 + c2, sub - 2, big,
                               tuple(big.t[k * 2 + sub - 2] for k in range(8)), 0)
                t = x_ring.get()
                ACT(lambda e, t=t, bk=bk, ct=ct: e.activation(
                    out=t.h[:], in_=bk.h[:], func=AF.Identity, scale=cwf(0, ct), bias=pcol(PV_CB + ct)),
                    r=(bk.tok, pvec.tok), w=(t.tok,))
                ACT(lambda e, bk=bk, hout=hout: e.activation(out=hout, in_=bk.h[:, 509:512], func=AF.Copy),
                    r=(bk.tok,), w=(hout_t,))
                u["t"].append(t)
                u["bk"].append(bk)
                u["hin"].append((hin, hin_t))
            return u

        def p1_conv(u, c2):
            ct = u["hd"] * 2 + c2
            t, bk = u["t"][c2], u["bk"][c2]
            hin, hin_t = u["hin"][c2]
            for k in range(1, 4):
                DVE(lambda e, t=t, hin=hin, k=k, ct=ct: e.scalar_tensor_tensor(
                    out=t.h[:, 0:k], in0=hin[:, 3 - k:3], scalar=cwf(k, ct), in1=t.h[:, 0:k],
                    op0=ALU.mult, op1=ALU.add), r=(hin_t, t.tok), w=(t.tok,))
            for k in range(1, 4):
                DVE(lambda e, t=t, bk=bk, k=k, ct=ct: e.scalar_tensor_tensor(
                    out=t.h[:, k:512], in0=bk.h[:, 0:512 - k], scalar=cwf(k, ct), in1=t.h[:, k:512],
                    op0=ALU.mult, op1=ALU.add), r=(bk.tok, t.tok), w=(t.tok,))

        def p1_cast(u, c2):
            t = u["t"][c2]
            x16 = xc16_ring.get()
            P.dma("pool", (lambda e, t=t, x16=x16: e.dma_start(out=x16.h[:], in_=t.h[:])), xcsems[xc16.index(x16)],
                  r=(t.tok,), w=(x16.tok,))
            u["x16"].append(x16)

        def p2(u):
            hd = u["hd"]
            u["rp"], u["ip"], u["a32"] = [], [], []
            xcb = u["x16"]
            for c2 in range(2):
                ct = hd * 2 + c2
                gb = []
                for g in range(2):
                    bk = bank_ring.get()
                    for kt in range(2):
                        PE(lambda e, bk=bk, g=g, kt=kt, c2=c2, hd=hd, xk=xcb[kt]: e.matmul(
                            bk.h[:], lhsT=wg.h[:, g * 4 + hd, kt * 256 + c2 * 128:kt * 256 + (c2 + 1) * 128],
                            rhs=xk.h[:], start=(kt == 0), stop=(kt == 1)),
                           r=(wg.tok, xcb[kt].tok), w=(bk.tok,), sig=(kt == 1))
                    gb.append(bk)
                rp = x_ring.get()
                ip = x_ring.get()
                a32 = x_ring.get()
                ACT(lambda e, bk=gb[0], rp=rp, ct=ct: e.activation(
                    out=rp.h[:], in_=bk.h[:], func=AF.Tanh, scale=0.5, bias=pv2.h[:, HBR + ct:HBR + ct + 1]),
                    r=(gb[0].tok, pv2.tok), w=(rp.tok,))
                ACT(lambda e, bk=gb[1], ip=ip, ct=ct: e.activation(
                    out=ip.h[:], in_=bk.h[:], func=AF.Tanh, scale=0.5, bias=pv2.h[:, HBI + ct:HBI + ct + 1]),
                    r=(gb[1].tok, pv2.tok), w=(ip.tok,))
                ACT(lambda e, rp=rp, a32=a32, ct=ct: e.activation(
                    out=a32.h[:], in_=rp.h[:], func=AF.Exp, scale=pv2.h[:, HSC + ct:HSC + ct + 1],
                    bias=pv2.h[:, HSC + ct:HSC + ct + 1]), r=(rp.tok, pv2.tok), w=(a32.tok,))
                ACT(lambda e, rp=rp, ct=ct: e.activation(
                    out=rp.h[:], in_=rp.h[:], func=AF.Exp, scale=pv2.h[:, SC + ct:SC + ct + 1],
                    bias=pv2.h[:, SC + ct:SC + ct + 1]), r=(rp.tok, pv2.tok), w=(rp.tok,))
                u["rp"].append(rp)
                u["ip"].append(ip)
                u["a32"].append(a32)

        def p3_t1(u):
            for c2 in range(2):
                DVE(lambda e, ip=u["ip"][c2], xc=u["t"][c2]: e.scalar_tensor_tensor(
                    out=ip.h[:], in0=ip.h[:], scalar=1.0, in1=xc.h[:], op0=ALU.add, op1=ALU.mult),
                    r=(u["ip"][c2].tok, u["t"][c2].tok), w=(u["ip"][c2].tok,))

        def p3_sqrt_mult(u):
            for c2 in range(2):
                rp = u["rp"][c2]
                ACT(lambda e, rp=rp: e.activation(out=rp.h[:], in_=rp.h[:], func=AF.Sqrt, scale=-0.25, bias=0.25),
                    r=(rp.tok,), w=(rp.tok,))
            for c2 in range(2):
                rp, ip = u["rp"][c2], u["ip"][c2]
                POOL(lambda e, rp=rp, ip=ip: e.tensor_tensor(out=ip.h[:], in0=ip.h[:], in1=rp.h[:], op=ALU.mult),
                     r=(rp.tok, ip.tok), w=(ip.tok,))

        def p3_scan(u, c2):
            hd, sub = u["hd"], u["sub"]
            ct = hd * 2 + c2
            rp, ip, a32 = u["rp"][c2], u["ip"][c2], u["a32"][c2]
            DVE(lambda e, rp=rp, ip=ip, a32=a32, ct=ct: e.tensor_tensor_scan(
                out=rp.h[:], data0=a32.h[:], data1=ip.h[:], initial=state.h[:, ct:ct + 1],
                op0=ALU.mult, op1=ALU.add), r=(a32.tok, ip.tok, state.t[ct]), w=(rp.tok,))
            POOL(lambda e, rp=rp, ct=ct: e.tensor_copy(out=state.h[:, ct:ct + 1], in_=rp.h[:, 511:512]),
                 r=(rp.tok,), w=(state.t[ct],))

        def p3_ya(u, c2):
            hd, sub = u["hd"], u["sub"]
            ct = hd * 2 + c2
            rp = u["rp"][c2]
            if main:
                POOL(lambda e, rp=rp, ct=ct, sub=sub: e.tensor_tensor(
                    out=big.h[:, ct, sub * 512:(sub + 1) * 512], in0=rp.h[:],
                    in1=gm.h[:, ct, sub * 512:(sub + 1) * 512], op=ALU.mult),
                    r=(rp.tok, gm.t[ct * 2 + sub]), w=(big.t[ct * 2 + sub],))

        units = [(hd, sub) for hd in range(4) for sub in range(nsub)]
        nu = len(units)
        us = {}
        for k in range(nu + 2):
            cur = None
            if k < nu:
                hd, sub = units[k]
                cur = us[k] = p1_front(hd, sub)
                if sub == nsub - 1 and hd % 2 == 1:
                    w_done()
            old = us.get(k - 2) if 0 <= k - 2 < nu else None
            if old is not None:
                p3_t1(old)
                p3_sqrt_mult(old)
            for c2 in range(2):
                if cur is not None:
                    p1_conv(cur, c2)
                    p1_cast(cur, c2)
                if old is not None:
                    p3_scan(old, c2)
            if old is not None:
                for c2 in range(2):
                    p3_ya(old, c2)
            if 0 <= k - 1 < nu:
                p2(us[k - 1])
            if old is not None:
                del us[k - 2]
        w_done()

    def stage_flag():
        DVE(lambda e: e.tensor_scalar(out=state.h[:], in0=state.h[:], scalar1=pcol(PV_FLAG), scalar2=None,
                                      op0=ALU.mult), r=tuple(state.t) + (pvec.tok,), w=tuple(state.t))
        DVE(lambda e: e.tensor_scalar(out=halo.h[:].rearrange("p a b c -> p (a b c)"),
                                      in0=halo.h[:].rearrange("p a b c -> p (a b c)"), scalar1=pcol(PV_FLAG),
                                      scalar2=None, op0=ALU.mult), r=tuple(halo.t) + (pvec.tok,), w=tuple(halo.t))

    mg16 = gm
    def mg_tok(ct, sub):
        return gm.t[ct * 2 + sub]

    def stage_c(tag, after_tile=None):
        for hf in range(2):
            wma = w_get((tag, "ma", hf))
            woa = w_get((tag, "oa", hf))
            wmb = w_get((tag, "mb", hf))
            wob = w_get((tag, "ob", hf))
            for sub in range(NSUB):
                ya_t = tuple(big.t[k * 2 + sub] for k in range(8))
                yb_t = tuple(big.t[16 + k * 2 + sub] for k in range(8))
                m1s = []
                for ci in range(4):
                    b_ma = mm_fm(wma, ci, sub)
                    b_pa = mm_fm(woa, ci, sub, big, ya_t, 0)
                    ta = t32_ring.get()
                    ACT(lambda e, b=b_ma, t=ta: e.activation(out=t.h[:], in_=b.h[:], func=AF.Tanh, scale=0.5),
                        r=(b_ma.tok,), w=(ta.tok,))
                    DVE(lambda e, t=ta, b=b_pa: e.scalar_tensor_tensor(
                        out=t.h[:], in0=t.h[:], scalar=1.0, in1=b.h[:], op0=ALU.add, op1=ALU.mult),
                        r=(ta.tok, b_pa.tok), w=(ta.tok,))
                    m1s.append(ta)
                if sub == NSUB - 1:
                    w_done(2)
                for ci in range(4):
                    ct = hf * 4 + ci
                    b_mb = mm_fm(wmb, ci, sub)
                    b_pb = mm_fm(wob, ci, sub, big, yb_t, 8)
                    ta = m1s[ci]
                    tb = t32_ring.get()
                    ACT(lambda e, b=b_mb, t=tb: e.activation(out=t.h[:], in_=b.h[:], func=AF.Tanh, scale=0.5),
                        r=(b_mb.tok,), w=(tb.tok,))
                    DVE(lambda e, t=tb, b=b_pb: e.scalar_tensor_tensor(
                        out=t.h[:], in0=t.h[:], scalar=1.0, in1=b.h[:], op0=ALU.add, op1=ALU.mult),
                        r=(tb.tok, b_pb.tok), w=(tb.tok,))
                    POOL(lambda e, ta=ta, tb=tb, ct=ct, sub=sub: e.tensor_tensor(
                        out=mg16.h[:, ct, sub * 512:(sub + 1) * 512], in0=ta.h[:], in1=tb.h[:], op=ALU.add),
                        r=(ta.tok, tb.tok), w=(mg_tok(ct, sub),))
            w_done(2)
        for hf in range(2):
            wo = w_get((tag, "o", hf))
            for tt in range(NTT):
                sub = tt // 4
                bk = bank_ring.get()
                mt = tuple(mg_tok(k, sub) for k in range(8))
                for kt in range(8):
                    PE(lambda e, bk=bk, kt=kt, tt=tt, wo=wo: e.matmul(
                        bk.h[:], lhsT=mg16.h[:, kt, tt * 128:(tt + 1) * 128], rhs=wo.h[:, kt, :],
                        start=(kt == 0), stop=(kt == 7)),
                       r=(wo.tok,) + mt, w=(bk.tok,), sig=(kt == 7))
                DVE(lambda e, bk=bk, tt=tt, hf=hf: e.scalar_tensor_tensor(
                    out=h_tok.h[:, tt, hf * 512:(hf + 1) * 512], in0=bk.h[:], scalar=0.5,
                    in1=h_tok.h[:, tt, hf * 512:(hf + 1) * 512], op0=ALU.mult, op1=ALU.add),
                    r=(bk.tok, h_tok.t[tt]), w=(h_tok.t[tt],))
                if hf == 1 and after_tile is not None:
                    after_tile()
            w_done()

    def stage_ffn(tag, after_tile=None):
        for dh in range(2):
            for j in range(4):
                ws = w_get((tag, "up", dh * 4 + j))
                for sub in range(NSUB):
                    for ci in range(4):
                        jt = j * 4 + ci
                        bk = mm_fm(ws, ci, sub)
                        t = t32_ring.get()
                        ACT(lambda e, bk=bk, t=t: e.activation(out=t.h[:], in_=bk.h[:], func=AF.Relu),
                            r=(bk.tok,), w=(t.tok,))
                        sq = (lambda e, t=t, jt=jt, sub=sub: e.tensor_tensor(
                            out=a16.h[:, jt, sub * 512:(sub + 1) * 512], in0=t.h[:], in1=t.h[:], op=ALU.mult))
                        if ci % 2 == 0:
                            POOL(sq, r=(t.tok,), w=(a16.t[jt * 2 + sub],))
                        else:
                            DVE(sq, r=(t.tok,), w=(a16.t[jt * 2 + sub],))
                w_done()
            for hf in range(2):
                wd = [w_get((tag, "dn", dh, hf, q)) for q in range(2)]
                for tt in range(NTT):
                    sub = tt // 4
                    bk = bank_ring.get()
                    for kt in range(16):
                        PE(lambda e, bk=bk, kt=kt, tt=tt, wk=wd[kt // 8]: e.matmul(
                            bk.h[:], lhsT=a16.h[:, kt, tt * 128:(tt + 1) * 128], rhs=wk.h[:, kt % 8, :],
                            start=(kt == 0), stop=(kt == 15)),
                           r=(wd[kt // 8].tok, a16.t[kt * 2 + sub]), w=(bk.tok,), sig=(kt == 15))
                    DVE(lambda e, bk=bk, tt=tt, hf=hf: e.tensor_tensor(
                        out=h_tok.h[:, tt, hf * 512:(hf + 1) * 512], in0=bk.h[:],
                        in1=h_tok.h[:, tt, hf * 512:(hf + 1) * 512], op=ALU.add),
                        r=(bk.tok, h_tok.t[tt]), w=(h_tok.t[tt],))
                    if dh == 1 and hf == 1 and after_tile is not None:
                        after_tile()
                w_done(2)

    out_evs = []

    def out_emitter(blk, next_x=None):
        bcf = bc[bcr["b"]]
        sts = {}
        kk = [0]

        def step():
            k = kk[0]
            kk[0] += 1
            if k < NTT:
                sts[k] = rms_stats(k)
            if 0 <= k - 1 < NTT:
                tt = k - 1
                col, st = sts.pop(tt)
                oi = ostage_ring.get()
                ogv = gm32[:, 2 * oi:2 * oi + 2, :]
                ogt = tuple(gm.t[4 * oi:4 * oi + 4])
                DVE(lambda e, ogv=ogv, col=col, tt=tt: e.scalar_tensor_tensor(
                    out=ogv, in0=h_tok.h[:, tt, :].rearrange("p (a b) -> p a b", a=2), scalar=col,
                    in1=bcf.h[:].rearrange("p (a b) -> p a b", a=2),
                    op0=ALU.mult, op1=ALU.mult), r=(h_tok.t[tt], st, bcf.tok), w=ogt)
                r0 = blk * TB + tt * 128
                dst = out_d.ap()[r0:r0 + 128, :].rearrange("p (a b) -> p a b", a=2)
                ev = P.dma("sp", (lambda e, ogv=ogv, dst=dst: e.dma_start(out=dst, in_=ogv)), osems[oi],
                           r=ogt, w=())
                out_evs.append(ev)
                if next_x is not None:
                    load_x_tile(next_x[0], next_x[1], tt)
        return step

    load_bc(bcr["b"], BV_LNG)
    load_bc(bcr["c"], BV_LNB)
    for kind, bi in blocks:
        tag = f"{kind}{bi}"
        if kind == "pre":
            stage_norm_to_fm(bcr["a"])
            w_prefetch(3)
            setup_compute()
            stage_norm_to_fm(bcr["a"], to_big=True, srcs=pre2_src)
            stage_xa(tag, False, nsub=4)
        else:
            if bi == 0:
                stage_flag()
                stage_load_x(x_main, bi)
            stage_norm_to_fm(bcr["a"])
            load_bc(bcr["a"], BV_G2)
            stage_b(tag)
            stage_ga(tag)
            stage_xa(tag, True)
            pre = {}
            def n2_stat(pre=pre):
                pre[len(pre)] = rms_stats(len(pre))
            stage_c(tag, after_tile=n2_stat)
            n2 = norm_emitter(bcr["a"], pre=pre)
            for _ in range(NTT + 2):
                n2()
            load_bc(bcr["b"], BV_GF)
            load_bc(bcr["c"], BV_G1)
            load_bc(bcr["a"], BV_LNG)
            oe = out_emitter(bi, (x_main, bi + 1) if bi + 1 < 2 else None)
            stage_ffn(tag, after_tile=oe)
            oe()
            load_bc(bcr["b"], BV_LNB)
            bcr["a"], bcr["b"], bcr["c"] = bcr["c"], bcr["a"], bcr["b"]
    assert wq_ptr[0] == len(wq) and w_consumed[0] == len(wq), (wq_ptr[0], w_consumed[0], len(wq))
    P.wait_all("sp", out_evs)

    with nc.Block() as block:
        @block.sync
        def _(e):
            Prog.run(P.eng["sp"], e)

        @block.gpsimd
        def _(e):
            Prog.run(P.eng["pool"], e)

        @block.scalar
        def _(e):
            Prog.run(P.eng["act"], e)

        @block.vector
        def _(e):
            Prog.run(P.eng["dve"], e)

        @block.tensor
        def _(e):
            Prog.run(P.eng["pe"], e)

    print("instr counts:", {k: len(v.q) for k, v in P.eng.items()})
    return nc


def _pack_pvec(conv_w, conv_b, b_r, b_i, lam, flag):
    def fm(v):
        return np.ascontiguousarray(v.reshape(8, 128).T)
    pv = np.zeros((128, PV_N), np.float32)
    for k in range(4):
        pv[:, PV_CW + k * 8:PV_CW + k * 8 + 8] = fm(conv_w[k])
    pv[:, PV_CB:PV_CB + 8] = fm(conv_b)
    pv[:, PV_BR:PV_BR + 8] = fm(b_r.reshape(-1))
    pv[:, PV_BI:PV_BI + 8] = fm(b_i.reshape(-1))
    pv[:, PV_LAM:PV_LAM + 8] = fm(lam)
    pv[:, PV_FLAG] = flag
    return pv


def kernel(x, norm_mix_g, w_in, conv_w, conv_b, w_rgate, b_rgate, w_igate, b_igate,
           lru_lambda, w_out_a, sgu_ln_g, sgu_ln_b, sgu_w_s, sgu_b_s, w_out_b, w_out,
           norm_mlp_g, w_up, w_down, norm_final_g):
    f = lambda a: np.ascontiguousarray(np.asarray(a, dtype=np.float32))
    x = f(x)
    gstack = np.stack([f(w_rgate)[0], f(w_igate)[0]])
    gates_p = np.ascontiguousarray(gstack.reshape(2, 4, 2, 128, 256).transpose(3, 0, 1, 2, 4)).reshape(128, 4096)
    shared = {
        "w_in": f(w_in)[0], "gates_p": gates_p,
        "w_out_a": f(w_out_a)[0], "w_out_b": f(w_out_b)[0], "w_out": f(w_out)[0],
        "w_up": f(w_up)[0], "w_down": f(w_down)[0], "sgu_w_s": f(sgu_w_s)[0],
        "bvec": np.ascontiguousarray(np.concatenate([np.broadcast_to(v, (128, D)) for v in (
            f(norm_mix_g)[0], f(norm_mlp_g)[0], f(norm_final_g), f(sgu_ln_g)[0], f(sgu_ln_b)[0])], axis=0)),
        "bsrow": np.ascontiguousarray(f(sgu_b_s)[0].reshape(1, 512)),
        "tril": np.tril(np.ones((128, 128), np.float32)),
        "ident": np.eye(128, dtype=np.float32),
    }
    in_maps = []
    for c in range(NCORES):
        b, half = divmod(c, 2)
        m = dict(shared)
        m["x_main"] = np.ascontiguousarray(x[b, half * TCORE:(half + 1) * TCORE])
        m["x_pre"] = np.ascontiguousarray(x[b, 0:TCORE])
        m["pvec"] = _pack_pvec(f(conv_w)[0], f(conv_b)[0], f(b_rgate)[0], f(b_igate)[0], f(lru_lambda)[0],
                               float(half))
        in_maps.append(m)
    nc = build_program()
    res = run_bass_kernel_spmd(nc, in_maps, core_ids=list(range(NCORES)))
    out = np.empty((BATCH, SEQ, D), np.float32)
    for c in range(NCORES):
        b, half = divmod(c, 2)
        out[b, half * TCORE:(half + 1) * TCORE] = res.results[c]["out"]
    return out
```

```python
import numpy as np
import concourse.bass as bass
import concourse.mybir as mybir
from concourse.bass_utils import run_bass_kernel_spmd

F32 = mybir.dt.float32
BF16 = mybir.dt.bfloat16
AF = mybir.ActivationFunctionType
ALU = mybir.AluOpType

D = 1024
SEQ = 4096
BATCH = 4
NCORES = 8
TCORE = 2048
TB = 1024
NTT = TB // 128
NSUB = TB // 512
DFF = 4096
NSLOT = 6
N_T32 = 12
NORM_EPS = 1e-6
LN_EPS = 1e-5

PV_CW = 0
PV_CB = 32
PV_BR = 40
PV_BI = 48
PV_LAM = 56
PV_FLAG = 64
PV_N = 65
BV_G1, BV_G2, BV_GF, BV_LNG, BV_LNB = range(5)


class Ev:
    __slots__ = ("sem", "val")

    def __init__(self, sem, val):
        self.sem = sem
        self.val = val


class Tok:
    __slots__ = ("w", "r", "name")

    def __init__(self, name=""):
        self.w = None
        self.r = []
        self.name = name


class Eng:
    def __init__(self, name, sem):
        self.name = name
        self.sem = sem
        self.cnt = 0
        self.waited = {}
        self.q = []
        self.pend_r = []
        self.pend_w = []


class DmaSem:
    def __init__(self, h):
        self.h = h
        self.cnt = 0


class Prog:
    def __init__(self, nc):
        self.nc = nc
        self.eng = {}
        for n in ("pe", "act", "dve", "pool", "sp"):
            self.eng[n] = Eng(n, nc.alloc_semaphore(name="s_" + n))
        self.nsem = 5

    def dsem(self, name):
        self.nsem += 1
        return DmaSem(self.nc.alloc_semaphore(name=name))

    def _waits(self, E, r, w):
        deps = []
        for t in r:
            if t.w is not None:
                deps.append(t.w)
        for t in w:
            if t.w is not None:
                deps.append(t.w)
            deps.extend(t.r)
        best = {}
        for ev in deps:
            assert ev.val is not None
            if ev.sem is E.sem and E.name == "pe":
                continue
            if E.waited.get(ev.sem, 0) < ev.val:
                if ev.sem not in best or best[ev.sem].val < ev.val:
                    best[ev.sem] = ev
        for sem, ev in best.items():
            E.waited[sem] = ev.val
        return list(best.values())

    def op(self, e, fn, r=(), w=(), sig=True):
        E = self.eng[e]
        waits = self._waits(E, r, w)
        if not sig:
            E.pend_r.extend(r)
            E.pend_w.extend(w)
            E.q.append((waits, fn, None))
            return None
        E.cnt += 1
        ev = Ev(E.sem, E.cnt)
        E.q.append((waits, fn, (E.sem, 1)))
        for t in list(r) + E.pend_r:
            t.r.append(ev)
        for t in list(w) + E.pend_w:
            t.w = ev
            t.r = []
        E.pend_r = []
        E.pend_w = []
        return ev

    def dma(self, q, fn, sem, r=(), w=()):
        E = self.eng[q]
        waits = self._waits(E, r, w)
        sem.cnt += 16
        ev = Ev(sem.h, sem.cnt)
        E.q.append((waits, fn, (sem.h, 16)))
        for t in r:
            t.r.append(ev)
        for t in w:
            t.w = ev
            t.r = []
        return ev

    def wait_all(self, q, evs):
        E = self.eng[q]
        waits = []
        for ev in evs:
            if E.waited.get(ev.sem, 0) < ev.val:
                E.waited[ev.sem] = ev.val
                waits.append(ev)
        E.q.append((waits, None, None))

    @staticmethod
    def run(E, e):
        for waits, fn, post in E.q:
            for ev in waits:
                e.wait_ge(ev.sem, ev.val)
            if fn is None:
                continue
            ins = fn(e)
            if post is not None:
                ins.then_inc(post[0], post[1])


class Tile:
    def __init__(self, h, ntok=1, name=""):
        self.h = h
        self.t = [Tok(name + str(i)) for i in range(ntok)]

    @property
    def tok(self):
        return self.t[0]


class Ring:
    def __init__(self, items):
        self.items = items
        self.i = 0

    def get(self):
        x = self.items[self.i % len(self.items)]
        self.i += 1
        return x


def build_program():
    nc = bass.Bass("TRN2", target_bir_lowering=False)
    P = Prog(nc)

    def din(name, shape):
        return nc.dram_tensor(name, list(shape), F32, kind="ExternalInput")

    x_main = din("x_main", [TCORE, D])
    x_pre = din("x_pre", [TCORE, D])
    w_in = din("w_in", [D, 6 * D])
    gates_d = din("gates_p", [128, 4096])
    w_oa = din("w_out_a", [D, D])
    w_ob = din("w_out_b", [D, D])
    w_o = din("w_out", [D, D])
    w_up = din("w_up", [D, DFF])
    w_dn = din("w_down", [DFF, D])
    w_s = din("sgu_w_s", [4, 128, 128])
    pvec_d = din("pvec", [128, PV_N])
    bvec_d = din("bvec", [5 * 128, D])
    bsrow_d = din("bsrow", [1, 512])
    tril_d = din("tril", [128, 128])
    ident_d = din("ident", [128, 128])
    out_d = nc.dram_tensor("out", [TCORE, D], F32, kind="ExternalOutput")

    def sb(name, shape, dt, ntok=1):
        return Tile(nc.alloc_sbuf_tensor(name, list(shape), dt), ntok, name)

    h_tok = sb("h_tok", [128, NTT, D], F32, NTT)
    n_fm = sb("n_fm", [128, 8, TB], BF16, NTT)
    wslots = [sb(f"wslot{i}", [128, 8, 512], BF16) for i in range(NSLOT)]
    wsems = [P.dsem(f"d_w{i}") for i in range(NSLOT)]
    bc = [sb(f"bc{i}", [128, D], F32) for i in range(3)]
    bcsems = [P.dsem(f"d_bc{i}") for i in range(3)]
    bcr = {"a": 0, "b": 1, "c": 2}
    pvec = sb("pvec_sb", [128, PV_N], F32)
    pv2 = sb("pv2", [128, 40], F32)
    ident16 = sb("ident16", [128, 128], BF16)
    negh = sb("negh", [128, 8], F32)
    wmt16 = sb("wmt16", [128, 4, 128], BF16)
    ones16 = sb("ones16", [2, 128], BF16)
    bs_hl = sb("bs_hl", [2, 512], BF16)
    bs_hi = sb("bs_hi", [2, 512], BF16)
    state = sb("state", [128, 8], F32, 8)
    halo = sb("halo", [128, 8, 2, 3], F32, 16)
    stat = sb("stat", [128, 16], F32)
    nstat = sb("nstat", [128, 32], F32, 8)
    stat_ring = Ring([0, 8])
    stat_tok = [Tok("st0"), Tok("st1")]
    junk16 = [sb(f"junk{i}", [128, D], BF16) for i in range(1)]
    junk_ring = Ring(junk16)

    big = sb("big", [128, 16, TB], BF16, 32)
    a16 = big
    gm = sb("gm", [128, 8, TB], BF16, 16)
    t32 = [sb(f"t32_{i}", [128, 512], F32) for i in range(N_T32)]
    t32_ring = Ring(t32)
    xc16 = [sb(f"xc16_{i}", [128, 512], BF16) for i in range(4)]
    xcsems = [P.dsem(f"d_xc{i}") for i in range(4)]
    xc16_ring = Ring(xc16)
    gv32 = [sb(f"gv32_{i}", [128, D], F32, 2) for i in range(2)]
    gv_ring = Ring(gv32)
    vln16 = [sb(f"vln16_{i}", [128, D], BF16) for i in range(4)]
    vln_ring = Ring(vln16)
    xn_ring = Ring(vln16)
    gm32 = gm.h.bitcast(F32)

    class V:
        def __init__(self, h, tok):
            self.h = h
            self.t = [tok]
            self.tok = tok
    xtiles = list(t32)
    for gvt in gv32:
        xtiles.append(V(gvt.h[:, 0:512], gvt.t[0]))
        xtiles.append(V(gvt.h[:, 512:1024], gvt.t[1]))
    for vt in vln16:
        xtiles.append(V(vt.h.bitcast(F32), vt.tok))
    x_ring = Ring(xtiles)
    sub_ctr = [0] * 8
    osems = [P.dsem(f"d_o{i}") for i in range(4)]
    ostage_ring = Ring(list(range(4)))
    xsems = [P.dsem(f"d_x{i}") for i in range(NTT)]
    csem = P.dsem("d_const")

    banks = [Tile(nc.alloc_psum_tensor(f"bank{i}", [128, 512], F32), 1, f"bank{i}") for i in range(8)]
    bank_ring = Ring(banks)

    print("sbuf bytes remaining/partition:", nc.sbuf_bytes_remaining, " sems used:", P.nsem)

    def ACT(fn, r=(), w=()):
        return P.op("act", fn, r, w)

    def DVE(fn, r=(), w=()):
        return P.op("dve", fn, r, w)

    def POOL(fn, r=(), w=()):
        return P.op("pool", fn, r, w)

    def PE(fn, r=(), w=(), sig=True):
        return P.op("pe", fn, r, w, sig)

    def pcol(c, n=1):
        return pvec.h[:, c:c + n]

    wq = []
    wq_ptr = [0]
    w_issued = {}
    slot_ctr = [0]

    def w_in_src(seg, half):
        c0 = seg * 1024 + half * 512
        return lambda: w_in.ap().rearrange("(kt p) c -> p kt c", p=128)[:, :, c0:c0 + 512]

    def sq_src(t, half):
        return lambda: t.ap().rearrange("(kt p) c -> p kt c", p=128)[:, :, half * 512:(half + 1) * 512]

    def up_src(j):
        return lambda: w_up.ap().rearrange("(kt p) c -> p kt c", p=128)[:, :, j * 512:(j + 1) * 512]

    def dn_src(kt0, half):
        return lambda: w_dn.ap().rearrange("(kt p) c -> p kt c", p=128)[:, kt0:kt0 + 8, half * 512:(half + 1) * 512]

    def w_prefetch(n_ahead, extra_r=()):
        while wq_ptr[0] < len(wq) and (wq_ptr[0] - w_consumed[0]) < n_ahead:
            key, srcf = wq[wq_ptr[0]]
            si = slot_ctr[0] % NSLOT
            slot_ctr[0] += 1
            slot = wslots[si]
            src = srcf()
            P.dma("pool", (lambda e, s=slot, a=src: e.dma_start(out=s.h[:], in_=a)), wsems[si], r=tuple(extra_r),
                  w=(slot.tok,))
            w_issued[key] = si
            wq_ptr[0] += 1

    w_consumed = [0]

    def w_get(key):
        if key not in w_issued:
            while key not in w_issued:
                assert wq_ptr[0] < len(wq), key
                w_prefetch(wq_ptr[0] - w_consumed[0] + 1)
        return wslots[w_issued[key]]

    def w_done(nkeys=1):
        w_consumed[0] += nkeys
        w_prefetch(NSLOT)

    SEG_XA, SEG_GA, SEG_U, SEG_V, SEG_MA, SEG_MB = range(6)
    blocks = [("pre", 0), ("main", 0), ("main", 1)]
    for kind, bi in blocks:
        tag = f"{kind}{bi}"
        if kind == "main":
            for hf in range(2):
                wq.append(((tag, "u", hf), w_in_src(SEG_U, hf)))
            for hf in range(2):
                wq.append(((tag, "v", hf), w_in_src(SEG_V, hf)))
            for hf in range(2):
                wq.append(((tag, "ga", hf), w_in_src(SEG_GA, hf)))
        for hf in range(2):
            wq.append(((tag, "xa", hf), w_in_src(SEG_XA, hf)))
        wq.append(((tag, "gates"), (lambda: gates_d.ap().rearrange("p (j m) -> p j m", j=8))))
        if kind == "main":
            for hf in range(2):
                wq.append(((tag, "ma", hf), w_in_src(SEG_MA, hf)))
                wq.append(((tag, "oa", hf), sq_src(w_oa, hf)))
                wq.append(((tag, "mb", hf), w_in_src(SEG_MB, hf)))
                wq.append(((tag, "ob", hf), sq_src(w_ob, hf)))
            for hf in range(2):
                wq.append(((tag, "o", hf), sq_src(w_o, hf)))
            for dh in range(2):
                for j in range(4):
                    wq.append(((tag, "up", dh * 4 + j), up_src(dh * 4 + j)))
                for hf in range(2):
                    for q in range(2):
                        wq.append(((tag, "dn", dh, hf, q), dn_src(dh * 16 + q * 8, hf)))

    const_evs = []

    def const_dma(q, out_ap, in_ap, wtoks):
        ev = P.dma(q, (lambda e: e.dma_start(out=out_ap, in_=in_ap)), csem, r=(), w=wtoks)
        assert all(wv.sem is not csem.h for wv in P.eng[q].q[-1][0]), "const DMA must not depend on a const DMA"
        const_evs.append(ev)
        return ev

    ident32 = V(t32[0].h[:, 0:128], t32[0].tok)
    tril32 = V(t32[1].h[:, 0:128], t32[1].tok)
    ws32 = V(t32[2].h.rearrange("p (g s) -> p g s", g=4), t32[2].tok)
    ws16 = V(xc16[0].h.rearrange("p (g s) -> p g s", g=4), xc16[0].tok)
    bs32 = V(t32[3].h[0:2, :], t32[3].tok)
    bsb32 = V(t32[4].h[0:2, :], t32[4].tok)

    def load_bc(i, row):
        src = bvec_d.ap()[row * 128:(row + 1) * 128, :]
        P.dma("sp", (lambda e: e.dma_start(out=bc[i].h[:], in_=src)), bcsems[i], r=(), w=(bc[i].tok,))

    isem = P.dsem("d_ident")
    P.dma("sp", (lambda e: e.dma_start(out=ident32.h[:], in_=ident_d.ap())), isem, r=(), w=(ident32.tok,))
    for tt in range(NTT):
        P.dma("sp", (lambda e, tt=tt: e.dma_start(out=h_tok.h[:, tt, :], in_=x_pre.ap()[tt * 128:(tt + 1) * 128, :])),
              xsems[tt], r=(), w=(h_tok.t[tt],))
    load_bc(0, BV_G1)
    bigf = big.h.bitcast(F32)
    pre2_src = []
    x2sems = [P.dsem(f"d_x2_{i}") for i in range(NTT)]
    for i in range(NTT):
        if i < 4:
            ap3 = gm32[:, 2 * i:2 * i + 2, :]
            toks = tuple(gm.t[4 * i:4 * i + 4])
        else:
            j = i - 4
            ap3 = bigf[:, 8 + 2 * j:8 + 2 * j + 2, :]
            toks = tuple(big.t[16 + 4 * j:16 + 4 * j + 4])
        r0 = TB + i * 128
        P.dma("sp", (lambda e, ap3=ap3, r0=r0: e.dma_start(
            out=ap3, in_=x_pre.ap()[r0:r0 + 128, :].rearrange("p (a b) -> p a b", a=2))), x2sems[i], r=(), w=toks)
        pre2_src.append((ap3.rearrange("p a b -> p (a b)"), toks))
    const_dma("sp", pvec.h[:], pvec_d.ap(), (pvec.tok,))
    const_dma("sp", tril32.h[:], tril_d.ap(), (tril32.tok,))
    const_dma("sp", ws32.h[:], w_s.ap().rearrange("g t s -> t g s"), (ws32.tok,))
    bs32b_tok = Tok("bs32b")
    const_dma("sp", bs32.h[0:1, :], bsrow_d.ap(), (bs32.tok,))
    const_dma("sp", bs32.h[1:2, :], bsrow_d.ap(), (bs32b_tok,))
    for ev in const_evs:
        ev.val = csem.cnt

    DVE(lambda e: e.tensor_copy(out=ident16.h[:], in_=ident32.h[:]), r=(ident32.tok,), w=(ident16.tok,))
    DVE(lambda e: e.memset(negh.h[:], -0.5), w=(negh.tok,))
    HBR, HBI, SC, HSC = 0, 8, 16, 24
    pv3 = sb("pv3", [128, 24], F32)

    def setup_compute():
      if True:
        DVE(lambda e: e.tensor_scalar(out=pv2.h[:, 0:16], in0=pvec.h[:, PV_BR:PV_BR + 16], scalar1=0.5, scalar2=None,
                                      op0=ALU.mult), r=(pvec.tok,), w=(pv2.tok,))
        ACT(lambda e: e.activation(out=pv2.h[:, 32:40], in_=pvec.h[:, PV_LAM:PV_LAM + 8], func=AF.Exp, scale=-1.0),
            r=(pvec.tok, pv2.tok), w=(pv2.tok,))
        DVE(lambda e: e.tensor_scalar(out=pv3.h[:, 0:8], in0=pv2.h[:, 32:40], scalar1=1.0, scalar2=None, op0=ALU.add),
            r=(pv2.tok,), w=(pv3.tok,))
        DVE(lambda e: e.tensor_scalar(out=pv3.h[:, 8:16], in0=pv3.h[:, 0:8], scalar1=-1.0, scalar2=1e-30, op0=ALU.add,
                                      op1=ALU.max), r=(pv3.tok,), w=(pv3.tok,))
        DVE(lambda e: e.reciprocal(out=pv3.h[:, 8:16], in_=pv3.h[:, 8:16]), r=(pv3.tok,), w=(pv3.tok,))
        ACT(lambda e: e.activation(out=pv3.h[:, 16:24], in_=pv3.h[:, 0:8], func=AF.Ln),
            r=(pv3.tok,), w=(pv3.tok,))
        DVE(lambda e: e.tensor_tensor(out=pv3.h[:, 16:24], in0=pv3.h[:, 16:24], in1=pv3.h[:, 8:16], op=ALU.mult),
            r=(pv3.tok,), w=(pv3.tok,))
        DVE(lambda e: e.tensor_tensor(out=pv2.h[:, 32:40], in0=pv3.h[:, 16:24], in1=pv2.h[:, 32:40], op=ALU.mult),
            r=(pv3.tok, pv2.tok), w=(pv2.tok,))
        DVE(lambda e: e.tensor_scalar(out=pv2.h[:, 16:24], in0=pv2.h[:, 32:40], scalar1=-8.0, scalar2=None, op0=ALU.mult),
            r=(pv2.tok,), w=(pv2.tok,))
        DVE(lambda e: e.tensor_scalar(out=pv2.h[:, 24:32], in0=pv2.h[:, 32:40], scalar1=-4.0, scalar2=None, op0=ALU.mult),
            r=(pv2.tok,), w=(pv2.tok,))

        for g in range(4):
            DVE(lambda e, g=g: e.tensor_tensor(out=ws16.h[:, g, :], in0=ws32.h[:, g, :], in1=tril32.h[:], op=ALU.mult),
                r=(ws32.tok, tril32.tok), w=(ws16.tok,))
        for g in range(4):
            bk = bank_ring.get()
            bv = bk.h.bitcast(BF16)
            PE(lambda e, g=g, bv=bv: e.transpose(bv[:, 0:128], ws16.h[:, g, :], ident16.h[:]),
               r=(ws16.tok, ident16.tok), w=(bk.tok,))
            ACT(lambda e, g=g, bv=bv: e.activation(out=wmt16.h[:, g, :], in_=bv[:, 0:128], func=AF.Copy),
                r=(bk.tok,), w=(wmt16.tok,))
        DVE(lambda e: e.tensor_copy(out=bs_hi.h[:], in_=bs32.h[:]), r=(bs32.tok, bs32b_tok), w=(bs_hi.tok,))
        DVE(lambda e: e.tensor_copy(out=bsb32.h[:], in_=bs_hi.h[:]), r=(bs_hi.tok,), w=(bsb32.tok,))
        DVE(lambda e: e.tensor_tensor(out=bsb32.h[:], in0=bs32.h[:], in1=bsb32.h[:], op=ALU.subtract),
            r=(bs32.tok, bs32b_tok, bsb32.tok), w=(bsb32.tok,))
        DVE(lambda e: e.tensor_copy(out=bs_hl.h[:], in_=bsb32.h[:]), r=(bsb32.tok,), w=(bs_hl.tok,))
        DVE(lambda e: e.tensor_copy(out=bs_hl.h[0:1, :], in_=bs_hi.h[0:1, :]), r=(bs_hi.tok, bs_hl.tok), w=(bs_hl.tok,))
        DVE(lambda e: e.memset(ones16.h[:], 1.0), w=(ones16.tok,))
        DVE(lambda e: e.memset(state.h[:], 0.0), w=tuple(state.t))
        DVE(lambda e: e.memset(halo.h[:], 0.0), w=tuple(halo.t))

    def load_x_tile(xsrc, blk, tt):
        r0 = blk * TB + tt * 128
        src = xsrc.ap()[r0:r0 + 128, :]
        P.dma("sp", (lambda e, tt=tt, a=src: e.dma_start(out=h_tok.h[:, tt, :], in_=a)), xsems[tt],
              r=(), w=(h_tok.t[tt],))

    def stage_load_x(xsrc, blk):
        for tt in range(NTT):
            load_x_tile(xsrc, blk, tt)

    def rms_stats(tt, src=None):
        base = tt * 4
        st = nstat.t[tt]
        jk = junk_ring.get()
        sap, stk = src if src is not None else (h_tok.h[:, tt, :], (h_tok.t[tt],))
        ACT(lambda e: e.activation(out=jk.h[:], in_=sap, func=AF.Square,
                                   accum_out=nstat.h[:, base:base + 1]),
            r=tuple(stk), w=(jk.tok, st))
        DVE(lambda e: e.tensor_scalar(out=nstat.h[:, base + 1:base + 2], in0=nstat.h[:, base:base + 1],
                                      scalar1=1.0 / D, scalar2=NORM_EPS, op0=ALU.mult, op1=ALU.add),
            r=(st,), w=(st,))
        POOL(lambda e: e.tensor_tensor(out=nstat.h[:, base + 2:base + 3], in0=nstat.h[:, base + 1:base + 2],
                                       in1=negh.h[:, 0:1], op=ALU.pow),
             r=(st, negh.tok), w=(st,))
        return nstat.h[:, base + 2:base + 3], st

    def norm_emitter(bci, to_big=False, pre=None, srcs=None):
        dst = big if to_big else n_fm
        def dtoks(tt):
            return tuple(big.t[k * 2 + tt // 4] for k in range(8)) if to_big else (n_fm.t[tt],)
        sts = {}
        trs = {}
        kk = [0]

        def step():
            k = kk[0]
            kk[0] += 1
            if k < NTT:
                sts[k] = pre[k] if pre is not None else rms_stats(k, srcs[k] if srcs is not None else None)
            if 0 <= k - 1 < NTT:
                tt = k - 1
                col, st = sts.pop(tt)
                xn = xn_ring.get()
                sap, stk = srcs[tt] if srcs is not None else (h_tok.h[:, tt, :], (h_tok.t[tt],))
                DVE(lambda e, xn=xn, col=col, sap=sap: e.scalar_tensor_tensor(
                    out=xn.h[:], in0=sap, scalar=col, in1=bc[bci].h[:],
                    op0=ALU.mult, op1=ALU.mult),
                    r=tuple(stk) + (st, bc[bci].tok), w=(xn.tok,))
                bk = bank_ring.get()
                bv = bk.h.bitcast(BF16)
                for kt in range(8):
                    PE(lambda e, xn=xn, bv=bv, kt=kt: e.transpose(bv[:, kt * 128:(kt + 1) * 128],
                                                                 xn.h[:, kt * 128:(kt + 1) * 128], ident16.h[:]),
                       r=(xn.tok, ident16.tok), w=(bk.tok,), sig=(kt == 7))
                trs[tt] = (bk, bv)
            if 0 <= k - 2 < NTT:
                tt = k - 2
                bk, bv = trs.pop(tt)
                bv3 = bv.rearrange("p (k t) -> p k t", k=8)
                ACT(lambda e, bv3=bv3, tt=tt: e.activation(
                    out=dst.h[:, 0:4, tt * 128:(tt + 1) * 128], in_=bv3[:, 0:4, :], func=AF.Copy),
                    r=(bk.tok,), w=dtoks(tt))
                DVE(lambda e, bv3=bv3, tt=tt: e.tensor_copy(
                    out=dst.h[:, 4:8, tt * 128:(tt + 1) * 128], in_=bv3[:, 4:8, :]),
                    r=(bk.tok,), w=dtoks(tt))
        return step

    def stage_norm_to_fm(bci, to_big=False, srcs=None):
        step = norm_emitter(bci, to_big, srcs=srcs)
        for _ in range(NTT + 2):
            step()

    def n_toks(sub):
        return tuple(n_fm.t[sub * 4:(sub + 1) * 4])

    def mm_fm(wslot, ct_in, sub, rhs_tile=None, rhs_toks=None, kt_off=0):
        bk = bank_ring.get()
        for kt in range(8):
            if rhs_tile is None:
                rhs = n_fm.h[:, kt, sub * 512:(sub + 1) * 512]
                rt = n_toks(sub)
            else:
                rhs = rhs_tile.h[:, kt_off + kt, sub * 512:(sub + 1) * 512]
                rt = rhs_toks
            PE(lambda e, bk=bk, kt=kt, rhs=rhs: e.matmul(
                bk.h[:], lhsT=wslot.h[:, kt, ct_in * 128:(ct_in + 1) * 128], rhs=rhs,
                start=(kt == 0), stop=(kt == 7)),
               r=(wslot.tok,) + tuple(rt), w=(bk.tok,), sig=(kt == 7))
        return bk

    def stage_ga(tag):
        for hf in range(2):
            ws = w_get((tag, "ga", hf))
            for sub in range(NSUB):
                for ci in range(4):
                    ct = hf * 4 + ci
                    bk = mm_fm(ws, ci, sub)
                    ACT(lambda e, bk=bk, ct=ct, sub=sub: e.activation(
                        out=gm.h[:, ct, sub * 512:(sub + 1) * 512], in_=bk.h[:], func=AF.Gelu_apprx_tanh),
                        r=(bk.tok,), w=(gm.t[ct * 2 + sub],))
            w_done()

    def stage_b(tag):
        bcg = bc[bcr["b"]]
        bcb = bc[bcr["c"]]
        wu = [w_get((tag, "u", hf)) for hf in range(2)]
        wv = [w_get((tag, "v", hf)) for hf in range(2)]

        def v_front(tt):
            gv = gv_ring.get()
            gi = stat_ring.i % 2
            base = stat_ring.get()
            st = stat_tok[gi]
            for hf in range(2):
                bk = bank_ring.get()
                for kt in range(8):
                    PE(lambda e, bk=bk, kt=kt, wvh=wv[hf], tt=tt: e.matmul(
                        bk.h[:], lhsT=n_fm.h[:, kt, tt * 128:(tt + 1) * 128], rhs=wvh.h[:, kt, :],
                        start=(kt == 0), stop=(kt == 7)),
                       r=(wv[hf].tok, n_fm.t[tt]), w=(bk.tok,), sig=(kt == 7))
                ACT(lambda e, bk=bk, gv=gv, hf=hf, base=base: e.activation(
                    out=gv.h[:, hf * 512:(hf + 1) * 512], in_=bk.h[:], func=AF.Gelu_apprx_tanh,
                    accum_out=stat.h[:, base + hf:base + hf + 1]),
                    r=(bk.tok,), w=(gv.t[hf], st))
            jk = junk_ring.get()
            ACT(lambda e, gv=gv, jk=jk, base=base: e.activation(
                out=jk.h[:], in_=gv.h[:], func=AF.Square, accum_out=stat.h[:, base + 2:base + 3]),
                r=tuple(gv.t), w=(jk.tok, st))
            b = base
            DVE(lambda e, b=b: e.tensor_tensor(out=stat.h[:, b + 3:b + 4], in0=stat.h[:, b:b + 1],
                                               in1=stat.h[:, b + 1:b + 2], op=ALU.add), r=(st,), w=(st,))
            DVE(lambda e, b=b: e.tensor_scalar(out=stat.h[:, b + 3:b + 4], in0=stat.h[:, b + 3:b + 4],
                                               scalar1=1.0 / D, scalar2=None, op0=ALU.mult), r=(st,), w=(st,))
            DVE(lambda e, b=b: e.tensor_tensor(out=stat.h[:, b + 4:b + 5], in0=stat.h[:, b + 3:b + 4],
                                               in1=stat.h[:, b + 3:b + 4], op=ALU.mult), r=(st,), w=(st,))
            DVE(lambda e, b=b: e.scalar_tensor_tensor(out=stat.h[:, b + 5:b + 6], in0=stat.h[:, b + 2:b + 3],
                                                      scalar=1.0 / D, in1=stat.h[:, b + 4:b + 5],
                                                      op0=ALU.mult, op1=ALU.subtract), r=(st,), w=(st,))
            DVE(lambda e, b=b: e.tensor_scalar(out=stat.h[:, b + 5:b + 6], in0=stat.h[:, b + 5:b + 6],
                                               scalar1=LN_EPS, scalar2=None, op0=ALU.add),
                r=(st,), w=(st,))
            POOL(lambda e, b=b: e.tensor_tensor(out=stat.h[:, b + 6:b + 7], in0=stat.h[:, b + 5:b + 6],
                                                in1=negh.h[:, 0:1], op=ALU.pow),
                 r=(st, negh.tok), w=(st,))
            DVE(lambda e, b=b: e.scalar_tensor_tensor(out=stat.h[:, b + 7:b + 8], in0=stat.h[:, b + 3:b + 4],
                                                      scalar=-1.0, in1=stat.h[:, b + 6:b + 7],
                                                      op0=ALU.mult, op1=ALU.mult), r=(st,), w=(st,))
            return (gv, b, st)

        def v_back(ctx):
            gv, b, st = ctx
            ACT(lambda e, gv=gv, b=b: e.activation(out=gv.h[:], in_=gv.h[:], func=AF.Identity,
                                                   scale=stat.h[:, b + 6:b + 7], bias=stat.h[:, b + 7:b + 8]),
                r=tuple(gv.t) + (st,), w=tuple(gv.t))
            DVE(lambda e, gv=gv: e.tensor_tensor(out=gv.h[:], in0=gv.h[:], in1=bcg.h[:], op=ALU.mult),
                r=tuple(gv.t) + (bcg.tok,), w=tuple(gv.t))
            v16 = vln_ring.get()
            POOL(lambda e, gv=gv, v16=v16: e.tensor_tensor(out=v16.h[:], in0=gv.h[:], in1=bcb.h[:], op=ALU.add),
                 r=tuple(gv.t) + (bcb.tok,), w=(v16.tok,))
            return v16

        for sub in range(NSUB):
            vl = []
            ctxs = {}
            for j in range(5):
                if j < 4:
                    ctxs[j] = v_front(sub * 4 + j)
                if j - 1 >= 0:
                    vl.append(v_back(ctxs.pop(j - 1)))
            gu = []
            for ct in range(8):
                bk = mm_fm(wu[ct // 4], ct % 4, sub)
                t = t32_ring.get()
                ACT(lambda e, bk=bk, t=t: e.activation(out=t.h[:], in_=bk.h[:], func=AF.Gelu_apprx_tanh),
                    r=(bk.tok,), w=(t.tok,))
                gu.append(t)
            for ct in range(8):
                g = ct // 2
                bk = bank_ring.get()
                brow = bass.AP(bs_hl.h, g * 128, [[512, 2], [0, 4], [1, 128]])
                PE(lambda e, bk=bk, brow=brow: e.matmul(
                    bk.h[:].rearrange("p (a b) -> p a b", a=4), lhsT=ones16.h[0:2, :], rhs=brow,
                    start=True, stop=False),
                   r=(ones16.tok, bs_hl.tok), w=(bk.tok,), sig=False)
                for j in range(4):
                    o = bk.h[:, j * 128:(j + 1) * 128]
                    PE(lambda e, o=o, v=vl[j], ct=ct, g=g, j=j: e.matmul(
                        o, lhsT=v.h[:, ct * 128:(ct + 1) * 128], rhs=wmt16.h[:, g, :], start=False, stop=(j == 3)),
                       r=(vl[j].tok, wmt16.tok), w=(bk.tok,), sig=(j == 3))
                DVE(lambda e, bk=bk, ct=ct, sub=sub, gt=gu[ct]: e.tensor_tensor(
                    out=big.h[:, 8 + ct, sub * 512:(sub + 1) * 512], in0=bk.h[:], in1=gt.h[:], op=ALU.mult),
                    r=(bk.tok, gu[ct].tok), w=(big.t[16 + ct * 2 + sub],))
        w_done(4)

    def stage_xa(tag, main, nsub=NSUB):
        wg = w_get((tag, "gates"))
        cwf = lambda k, ct: pcol(PV_CW + k * 8 + ct)

        def p1_front(hd, sub):
            ws = w_get((tag, "xa", hd // 2))
            u = {"hd": hd, "sub": sub, "t": [], "bk": [], "hin": [], "x16": []}
            for c2 in range(2):
                ct = hd * 2 + c2
                par = sub_ctr[ct] % 2
                sub_ctr[ct] += 1
                hin = halo.h[:, ct, par, :]
                hin_t = halo.t[ct * 2 + par]
                hout = halo.h[:, ct, 1 - par, :]
                hout_t = halo.t[ct * 2 + 1 - par]
                if sub < 2:
                    bk = mm_fm(ws, (hd % 2) * 2 + c2, sub)
                else:
                    bk = mm_fm(ws, (hd % 2) * 2 + c2, sub - 2, big,
                               tuple(big.t[k * 2 + sub - 2] for k in range(8)), 0)
                t = x_ring.get()
                ACT(lambda e, t=t, bk=bk, ct=ct: e.activation(
                    out=t.h[:], in_=bk.h[:], func=AF.Identity, scale=cwf(0, ct), bias=pcol(PV_CB + ct)),
                    r=(bk.tok, pvec.tok), w=(t.tok,))
                ACT(lambda e, bk=bk, hout=hout: e.activation(out=hout, in_=bk.h[:, 509:512], func=AF.Copy),
                    r=(bk.tok,), w=(hout_t,))
                u["t"].append(t)
                u["bk"].append(bk)
                u["hin"].append((hin, hin_t))
            return u

        def p1_taps(u, c2):
            ct = u["hd"] * 2 + c2
            t, bk = u["t"][c2], u["bk"][c2]
            for k in range(1, 4):
                DVE(lambda e, t=t, bk=bk, k=k, ct=ct: e.scalar_tensor_tensor(
                    out=t.h[:, k:512], in0=bk.h[:, 0:512 - k], scalar=cwf(k, ct), in1=t.h[:, k:512],
                    op0=ALU.mult, op1=ALU.add), r=(bk.tok, t.tok), w=(t.tok,))

        def p1_halo(u, c2):
            ct = u["hd"] * 2 + c2
            t = u["t"][c2]
            hin, hin_t = u["hin"][c2]
            for k in range(1, 4):
                DVE(lambda e, t=t, hin=hin, k=k, ct=ct: e.scalar_tensor_tensor(
                    out=t.h[:, 0:k], in0=hin[:, 3 - k:3], scalar=cwf(k, ct), in1=t.h[:, 0:k],
                    op0=ALU.mult, op1=ALU.add), r=(hin_t, t.tok), w=(t.tok,))

        def p1_cast(u, c2):
            t = u["t"][c2]
            x16 = xc16_ring.get()
            P.dma("pool", (lambda e, t=t, x16=x16: e.dma_start(out=x16.h[:], in_=t.h[:])), xcsems[xc16.index(x16)],
                  r=(t.tok,), w=(x16.tok,))
            u["x16"].append(x16)

        def p2(u):
            hd = u["hd"]
            u["rp"], u["ip"], u["a32"] = [], [], []
            xcb = u["x16"]
            for c2 in range(2):
                ct = hd * 2 + c2
                gb = []
                for g in range(2):
                    bk = bank_ring.get()
                    for kt in range(2):
                        PE(lambda e, bk=bk, g=g, kt=kt, c2=c2, hd=hd, xk=xcb[kt]: e.matmul(
                            bk.h[:], lhsT=wg.h[:, g * 4 + hd, kt * 256 + c2 * 128:kt * 256 + (c2 + 1) * 128],
                            rhs=xk.h[:], start=(kt == 0), stop=(kt == 1)),
                           r=(wg.tok, xcb[kt].tok), w=(bk.tok,), sig=(kt == 1))
                    gb.append(bk)
                rp = x_ring.get()
                ip = x_ring.get()
                a32 = x_ring.get()
                ACT(lambda e, bk=gb[0], rp=rp, ct=ct: e.activation(
                    out=rp.h[:], in_=bk.h[:], func=AF.Tanh, scale=0.5, bias=pv2.h[:, HBR + ct:HBR + ct + 1]),
                    r=(gb[0].tok, pv2.tok), w=(rp.tok,))
                ACT(lambda e, bk=gb[1], ip=ip, ct=ct: e.activation(
                    out=ip.h[:], in_=bk.h[:], func=AF.Tanh, scale=0.5, bias=pv2.h[:, HBI + ct:HBI + ct + 1]),
                    r=(gb[1].tok, pv2.tok), w=(ip.tok,))
                ACT(lambda e, rp=rp, a32=a32, ct=ct: e.activation(
                    out=a32.h[:], in_=rp.h[:], func=AF.Exp, scale=pv2.h[:, HSC + ct:HSC + ct + 1],
                    bias=pv2.h[:, HSC + ct:HSC + ct + 1]), r=(rp.tok, pv2.tok), w=(a32.tok,))
                ACT(lambda e, rp=rp, ct=ct: e.activation(
                    out=rp.h[:], in_=rp.h[:], func=AF.Exp, scale=pv2.h[:, SC + ct:SC + ct + 1],
                    bias=pv2.h[:, SC + ct:SC + ct + 1]), r=(rp.tok, pv2.tok), w=(rp.tok,))
                u["rp"].append(rp)
                u["ip"].append(ip)
                u["a32"].append(a32)

        def p3_t1(u):
            for c2 in range(2):
                DVE(lambda e, ip=u["ip"][c2], xc=u["t"][c2]: e.scalar_tensor_tensor(
                    out=ip.h[:], in0=ip.h[:], scalar=1.0, in1=xc.h[:], op0=ALU.add, op1=ALU.mult),
                    r=(u["ip"][c2].tok, u["t"][c2].tok), w=(u["ip"][c2].tok,))

        def p3_sqrt_mult(u):
            for c2 in range(2):
                rp = u["rp"][c2]
                ACT(lambda e, rp=rp: e.activation(out=rp.h[:], in_=rp.h[:], func=AF.Sqrt, scale=-0.25, bias=0.25),
                    r=(rp.tok,), w=(rp.tok,))
            for c2 in range(2):
                rp, ip = u["rp"][c2], u["ip"][c2]
                POOL(lambda e, rp=rp, ip=ip: e.tensor_tensor(out=ip.h[:], in0=ip.h[:], in1=rp.h[:], op=ALU.mult),
                     r=(rp.tok, ip.tok), w=(ip.tok,))

        def p3_scan(u, c2):
            hd, sub = u["hd"], u["sub"]
            ct = hd * 2 + c2
            rp, ip, a32 = u["rp"][c2], u["ip"][c2], u["a32"][c2]
            DVE(lambda e, rp=rp, ip=ip, a32=a32, ct=ct: e.tensor_tensor_scan(
                out=rp.h[:], data0=a32.h[:], data1=ip.h[:], initial=state.h[:, ct:ct + 1],
                op0=ALU.mult, op1=ALU.add), r=(a32.tok, ip.tok, state.t[ct]), w=(rp.tok,))
            POOL(lambda e, rp=rp, ct=ct: e.tensor_copy(out=state.h[:, ct:ct + 1], in_=rp.h[:, 511:512]),
                 r=(rp.tok,), w=(state.t[ct],))

        def p3_ya(u, c2):
            hd, sub = u["hd"], u["sub"]
            ct = hd * 2 + c2
            rp = u["rp"][c2]
            if main:
                POOL(lambda e, rp=rp, ct=ct, sub=sub: e.tensor_tensor(
                    out=big.h[:, ct, sub * 512:(sub + 1) * 512], in0=rp.h[:],
                    in1=gm.h[:, ct, sub * 512:(sub + 1) * 512], op=ALU.mult),
                    r=(rp.tok, gm.t[ct * 2 + sub]), w=(big.t[ct * 2 + sub],))

        units = [(hd, sub) for hd in range(4) for sub in range(nsub)]
        nu = len(units)
        us = {}
        for k in range(nu + 2):
            cur = None
            if k < nu:
                hd, sub = units[k]
                cur = us[k] = p1_front(hd, sub)
                if sub == nsub - 1 and hd % 2 == 1:
                    w_done()
            old = us.get(k - 2) if 0 <= k - 2 < nu else None
            if old is not None:
                p3_t1(old)
                p3_sqrt_mult(old)
            if cur is not None:
                for c2 in range(2):
                    p1_taps(cur, c2)
                for c2 in range(2):
                    p1_halo(cur, c2)
                    p1_cast(cur, c2)
            if old is not None:
                for c2 in range(2):
                    p3_scan(old, c2)
            if old is not None:
                for c2 in range(2):
                    p3_ya(old, c2)
            if 0 <= k - 1 < nu:
                p2(us[k - 1])
            if old is not None:
                del us[k - 2]
        w_done()

    def stage_flag():
        DVE(lambda e: e.tensor_scalar(out=state.h[:], in0=state.h[:], scalar1=pcol(PV_FLAG), scalar2=None,
                                      op0=ALU.mult), r=tuple(state.t) + (pvec.tok,), w=tuple(state.t))
        DVE(lambda e: e.tensor_scalar(out=halo.h[:].rearrange("p a b c -> p (a b c)"),
                                      in0=halo.h[:].rearrange("p a b c -> p (a b c)"), scalar1=pcol(PV_FLAG),
                                      scalar2=None, op0=ALU.mult), r=tuple(halo.t) + (pvec.tok,), w=tuple(halo.t))

    mg16 = gm
    def mg_tok(ct, sub):
        return gm.t[ct * 2 + sub]

    def stage_c(tag, after_tile=None):
        for hf in range(2):
            wma = w_get((tag, "ma", hf))
            woa = w_get((tag, "oa", hf))
            wmb = w_get((tag, "mb", hf))
            wob = w_get((tag, "ob", hf))
            for sub in range(NSUB):
                ya_t = tuple(big.t[k * 2 + sub] for k in range(8))
                yb_t = tuple(big.t[16 + k * 2 + sub] for k in range(8))
                m1s = []
                for ci in range(4):
                    b_ma = mm_fm(wma, ci, sub)
                    b_pa = mm_fm(woa, ci, sub, big, ya_t, 0)
                    ta = t32_ring.get()
                    ACT(lambda e, b=b_ma, t=ta: e.activation(out=t.h[:], in_=b.h[:], func=AF.Tanh, scale=0.5),
                        r=(b_ma.tok,), w=(ta.tok,))
                    DVE(lambda e, t=ta, b=b_pa: e.scalar_tensor_tensor(
                        out=t.h[:], in0=t.h[:], scalar=1.0, in1=b.h[:], op0=ALU.add, op1=ALU.mult),
                        r=(ta.tok, b_pa.tok), w=(ta.tok,))
                    m1s.append(ta)
                if sub == NSUB - 1:
                    w_done(2)
                for ci in range(4):
                    ct = hf * 4 + ci
                    b_mb = mm_fm(wmb, ci, sub)
                    b_pb = mm_fm(wob, ci, sub, big, yb_t, 8)
                    ta = m1s[ci]
                    tb = t32_ring.get()
                    ACT(lambda e, b=b_mb, t=tb: e.activation(out=t.h[:], in_=b.h[:], func=AF.Tanh, scale=0.5),
                        r=(b_mb.tok,), w=(tb.tok,))
                    DVE(lambda e, t=tb, b=b_pb: e.scalar_tensor_tensor(
                        out=t.h[:], in0=t.h[:], scalar=1.0, in1=b.h[:], op0=ALU.add, op1=ALU.mult),
                        r=(tb.tok, b_pb.tok), w=(tb.tok,))
                    POOL(lambda e, ta=ta, tb=tb, ct=ct, sub=sub: e.tensor_tensor(
                        out=mg16.h[:, ct, sub * 512:(sub + 1) * 512], in0=ta.h[:], in1=tb.h[:], op=ALU.add),
                        r=(ta.tok, tb.tok), w=(mg_tok(ct, sub),))
            w_done(2)
        for hf in range(2):
            wo = w_get((tag, "o", hf))
            for tt in range(NTT):
                sub = tt // 4
                bk = bank_ring.get()
                mt = tuple(mg_tok(k, sub) for k in range(8))
                for kt in range(8):
                    PE(lambda e, bk=bk, kt=kt, tt=tt, wo=wo: e.matmul(
                        bk.h[:], lhsT=mg16.h[:, kt, tt * 128:(tt + 1) * 128], rhs=wo.h[:, kt, :],
                        start=(kt == 0), stop=(kt == 7)),
                       r=(wo.tok,) + mt, w=(bk.tok,), sig=(kt == 7))
                DVE(lambda e, bk=bk, tt=tt, hf=hf: e.scalar_tensor_tensor(
                    out=h_tok.h[:, tt, hf * 512:(hf + 1) * 512], in0=bk.h[:], scalar=0.5,
                    in1=h_tok.h[:, tt, hf * 512:(hf + 1) * 512], op0=ALU.mult, op1=ALU.add),
                    r=(bk.tok, h_tok.t[tt]), w=(h_tok.t[tt],))
                if hf == 1 and after_tile is not None:
                    after_tile()
            w_done()

    def stage_ffn(tag, after_tile=None):
        for dh in range(2):
            for j in range(4):
                ws = w_get((tag, "up", dh * 4 + j))
                for sub in range(NSUB):
                    for ci in range(4):
                        jt = j * 4 + ci
                        bk = mm_fm(ws, ci, sub)
                        t = t32_ring.get()
                        ACT(lambda e, bk=bk, t=t: e.activation(out=t.h[:], in_=bk.h[:], func=AF.Relu),
                            r=(bk.tok,), w=(t.tok,))
                        sq = (lambda e, t=t, jt=jt, sub=sub: e.tensor_tensor(
                            out=a16.h[:, jt, sub * 512:(sub + 1) * 512], in0=t.h[:], in1=t.h[:], op=ALU.mult))
                        if ci % 2 == 0:
                            POOL(sq, r=(t.tok,), w=(a16.t[jt * 2 + sub],))
                        else:
                            DVE(sq, r=(t.tok,), w=(a16.t[jt * 2 + sub],))
                w_done()
            for hf in range(2):
                wd = [w_get((tag, "dn", dh, hf, q)) for q in range(2)]
                for tt in range(NTT):
                    sub = tt // 4
                    bk = bank_ring.get()
                    for kt in range(16):
                        PE(lambda e, bk=bk, kt=kt, tt=tt, wk=wd[kt // 8]: e.matmul(
                            bk.h[:], lhsT=a16.h[:, kt, tt * 128:(tt + 1) * 128], rhs=wk.h[:, kt % 8, :],
                            start=(kt == 0), stop=(kt == 15)),
                           r=(wd[kt // 8].tok, a16.t[kt * 2 + sub]), w=(bk.tok,), sig=(kt == 15))
                    DVE(lambda e, bk=bk, tt=tt, hf=hf: e.tensor_tensor(
                        out=h_tok.h[:, tt, hf * 512:(hf + 1) * 512], in0=bk.h[:],
                        in1=h_tok.h[:, tt, hf * 512:(hf + 1) * 512], op=ALU.add),
                        r=(bk.tok, h_tok.t[tt]), w=(h_tok.t[tt],))
                    if dh == 1 and hf == 1 and after_tile is not None:
                        after_tile()
                w_done(2)

    out_evs = []

    def out_emitter(blk, next_x=None):
        bcf = bc[bcr["b"]]
        sts = {}
        kk = [0]

        def step():
            k = kk[0]
            kk[0] += 1
            if k < NTT:
                sts[k] = rms_stats(k)
            if 0 <= k - 1 < NTT:
                tt = k - 1
                col, st = sts.pop(tt)
                oi = ostage_ring.get()
                ogv = gm32[:, 2 * oi:2 * oi + 2, :]
                ogt = tuple(gm.t[4 * oi:4 * oi + 4])
                DVE(lambda e, ogv=ogv, col=col, tt=tt: e.scalar_tensor_tensor(
                    out=ogv, in0=h_tok.h[:, tt, :].rearrange("p (a b) -> p a b", a=2), scalar=col,
                    in1=bcf.h[:].rearrange("p (a b) -> p a b", a=2),
                    op0=ALU.mult, op1=ALU.mult), r=(h_tok.t[tt], st, bcf.tok), w=ogt)
                r0 = blk * TB + tt * 128
                dst = out_d.ap()[r0:r0 + 128, :].rearrange("p (a b) -> p a b", a=2)
                ev = P.dma("sp", (lambda e, ogv=ogv, dst=dst: e.dma_start(out=dst, in_=ogv)), osems[oi],
                           r=ogt, w=())
                out_evs.append(ev)
                if next_x is not None:
                    load_x_tile(next_x[0], next_x[1], tt)
        return step

    load_bc(bcr["b"], BV_LNG)
    load_bc(bcr["c"], BV_LNB)
    for kind, bi in blocks:
        tag = f"{kind}{bi}"
        if kind == "pre":
            stage_norm_to_fm(bcr["a"])
            w_prefetch(3)
            setup_compute()
            stage_norm_to_fm(bcr["a"], to_big=True, srcs=pre2_src)
            stage_xa(tag, False, nsub=4)
        else:
            if bi == 0:
                stage_flag()
                stage_load_x(x_main, bi)
            stage_norm_to_fm(bcr["a"])
            load_bc(bcr["a"], BV_G2)
            stage_b(tag)
            stage_ga(tag)
            stage_xa(tag, True)
            pre = {}
            def n2_stat(pre=pre):
                pre[len(pre)] = rms_stats(len(pre))
            stage_c(tag, after_tile=n2_stat)
            n2 = norm_emitter(bcr["a"], pre=pre)
            for _ in range(NTT + 2):
                n2()
            load_bc(bcr["b"], BV_GF)
            load_bc(bcr["c"], BV_G1)
            load_bc(bcr["a"], BV_LNG)
            oe = out_emitter(bi, (x_main, bi + 1) if bi + 1 < 2 else None)
            stage_ffn(tag, after_tile=oe)
            oe()
            load_bc(bcr["b"], BV_LNB)
            bcr["a"], bcr["b"], bcr["c"] = bcr["c"], bcr["a"], bcr["b"]
    assert wq_ptr[0] == len(wq) and w_consumed[0] == len(wq), (wq_ptr[0], w_consumed[0], len(wq))
    P.wait_all("sp", out_evs)

    with nc.Block() as block:
        @block.sync
        def _(e):
            Prog.run(P.eng["sp"], e)

        @block.gpsimd
        def _(e):
            Prog.run(P.eng["pool"], e)

        @block.scalar
        def _(e):
            Prog.run(P.eng["act"], e)

        @block.vector
        def _(e):
            Prog.run(P.eng["dve"], e)

        @block.tensor
        def _(e):
            Prog.run(P.eng["pe"], e)

    print("instr counts:", {k: len(v.q) for k, v in P.eng.items()})
    return nc


def _pack_pvec(conv_w, conv_b, b_r, b_i, lam, flag):
    def fm(v):
        return np.ascontiguousarray(v.reshape(8, 128).T)
    pv = np.zeros((128, PV_N), np.float32)
    for k in range(4):
        pv[:, PV_CW + k * 8:PV_CW + k * 8 + 8] = fm(conv_w[k])
    pv[:, PV_CB:PV_CB + 8] = fm(conv_b)
    pv[:, PV_BR:PV_BR + 8] = fm(b_r.reshape(-1))
    pv[:, PV_BI:PV_BI + 8] = fm(b_i.reshape(-1))
    pv[:, PV_LAM:PV_LAM + 8] = fm(lam)
    pv[:, PV_FLAG] = flag
    return pv


def kernel(x, norm_mix_g, w_in, conv_w, conv_b, w_rgate, b_rgate, w_igate, b_igate,
           lru_lambda, w_out_a, sgu_ln_g, sgu_ln_b, sgu_w_s, sgu_b_s, w_out_b, w_out,
           norm_mlp_g, w_up, w_down, norm_final_g):
    f = lambda a: np.ascontiguousarray(np.asarray(a, dtype=np.float32))
    x = f(x)
    gstack = np.stack([f(w_rgate)[0], f(w_igate)[0]])
    gates_p = np.ascontiguousarray(gstack.reshape(2, 4, 2, 128, 256).transpose(3, 0, 1, 2, 4)).reshape(128, 4096)
    shared = {
        "w_in": f(w_in)[0], "gates_p": gates_p,
        "w_out_a": f(w_out_a)[0], "w_out_b": f(w_out_b)[0], "w_out": f(w_out)[0],
        "w_up": f(w_up)[0], "w_down": f(w_down)[0], "sgu_w_s": f(sgu_w_s)[0],
        "bvec": np.ascontiguousarray(np.concatenate([np.broadcast_to(v, (128, D)) for v in (
            f(norm_mix_g)[0], f(norm_mlp_g)[0], f(norm_final_g), f(sgu_ln_g)[0], f(sgu_ln_b)[0])], axis=0)),
        "bsrow": np.ascontiguousarray(f(sgu_b_s)[0].reshape(1, 512)),
        "tril": np.tril(np.ones((128, 128), np.float32)),
        "ident": np.eye(128, dtype=np.float32),
    }
    in_maps = []
    for c in range(NCORES):
        b, half = divmod(c, 2)
        m = dict(shared)
        m["x_main"] = np.ascontiguousarray(x[b, half * TCORE:(half + 1) * TCORE])
        m["x_pre"] = np.ascontiguousarray(x[b, 0:TCORE])
        m["pvec"] = _pack_pvec(f(conv_w)[0], f(conv_b)[0], f(b_rgate)[0], f(b_igate)[0], f(lru_lambda)[0],
                               float(half))
        in_maps.append(m)
    nc = build_program()
    res = run_bass_kernel_spmd(nc, in_maps, core_ids=list(range(NCORES)))
    out = np.empty((BATCH, SEQ, D), np.float32)
    for c in range(NCORES):
        b, half = divmod(c, 2)
        out[b, half * TCORE:(half + 1) * TCORE] = res.results[c]["out"]
    return out
```
